# Optimizing a Trainium2 kernel written in Bass

```python
import math
import jax, jax.numpy as jnp
from jax import lax
import numpy as np

D_MODEL = 2048
BATCH = 4
SEQ = 2048
DEPTH = 4
DEC_BATCH = 128
DEC_SEQ = 4
PAST_LEN = 16384
PAGE_SIZE = 128

N_MIXERS = 2
N_DELTA = (DEPTH + 1) // 2
N_POOL = DEPTH // 2
HEAD_DIM = 128
MIX_W = D_MODEL
N_MEM = 256
N_MEM_HEADS = 4
MEM_W = N_MEM_HEADS * HEAD_DIM
TOK_W = MIX_W - MEM_W
N_DELTA_HEADS = TOK_W // HEAD_DIM
QKV_W = 3 * TOK_W
CONV_W = 4
CHUNK = 64
POOL_WINDOWS = (2, 4, 8, 16)
N_POOL_GROUPS = len(POOL_WINDOWS)
POOL_GROUP_W = TOK_W // N_POOL_GROUPS
POOL_BUF = max(POOL_WINDOWS) - 1
D_FF = ((8 * D_MODEL + 3 * 256 - 1) // (3 * 256)) * 256
IN_DELTA = QKV_W + TOK_W + 2 * N_DELTA_HEADS + MEM_W
IN_POOL = TOK_W + MEM_W
EPS = 1e-6

kernel_name = 'hybrid_gdn_pool_memxattn_step'


def rmsnorm(x, w):
    xf = x.astype(jnp.float32)
    y = xf * lax.rsqrt(jnp.mean(xf * xf, axis=-1, keepdims=True) + EPS)
    return (y * w.astype(jnp.float32)).astype(x.dtype)


def l2norm(x):
    xf = x.astype(jnp.float32)
    return xf * lax.rsqrt(jnp.sum(xf * xf, axis=-1, keepdims=True) + EPS)


def causal_conv(x, buf, w):
    xp = jnp.concatenate([buf.astype(x.dtype), x], axis=1)
    y = lax.conv_general_dilated(xp, w[:, None, :].astype(x.dtype), window_strides=(1,), padding='VALID',
                                 dimension_numbers=('NWC', 'WIO', 'NWC'), feature_group_count=x.shape[-1])
    return jax.nn.silu(y), xp[:, -(CONV_W - 1):]


def gated_delta_chunked(q, k, v, g, beta, S0):
    B, L, H, _ = q.shape
    DV = v.shape[-1]
    c = min(CHUNK, L)
    pad = (-L) % c
    if pad:
        padf = lambda a: jnp.pad(a, [(0, 0), (0, pad)] + [(0, 0)] * (a.ndim - 2))
        q, k, v, g, beta = (padf(a) for a in (q, k, v, g, beta))
    n = (L + pad) // c
    q, k, v, g, beta = (jnp.moveaxis(a.reshape(B, n, c, H, *a.shape[3:]), 3, 1) for a in (q, k, v, g, beta))
    gc = jnp.cumsum(g, axis=-1)
    diff = gc[..., :, None] - gc[..., None, :]
    idx = jnp.arange(c)
    incl = idx[:, None] >= idx[None, :]
    strict = idx[:, None] > idx[None, :]
    decay = jnp.exp(jnp.where(incl, diff, -jnp.inf))
    kk = jnp.einsum('bhnik,bhnjk->bhnij', k, k)
    lmat = jnp.where(strict, kk * decay, 0.0) * beta[..., :, None]
    eye = jnp.eye(c, dtype=jnp.float32)
    gam = jnp.exp(gc)
    rhs = jnp.concatenate([v * beta[..., None], k * (beta * gam)[..., None]], axis=-1)
    sol = lax.linalg.triangular_solve(eye + lmat, rhs, left_side=True, lower=True, unit_diagonal=True)
    u_v, w_k = sol[..., :DV], sol[..., DV:]
    attn = jnp.einsum('bhnik,bhnjk->bhnij', q, k) * decay
    q_g = q * gam[..., None]
    k_end = k * jnp.exp(gc[..., -1:] - gc)[..., None]
    g_end = jnp.exp(gc[..., -1])

    def step(S, xs):
        qg_c, attn_c, uv_c, wk_c, kend_c, gend_c = xs
        u = uv_c - jnp.einsum('bhck,bhkv->bhcv', wk_c, S)
        o = jnp.einsum('bhck,bhkv->bhcv', qg_c, S) + jnp.einsum('bhij,bhjv->bhiv', attn_c, u)
        S = S * gend_c[..., None, None] + jnp.einsum('bhck,bhcv->bhkv', kend_c, u)
        return S, o

    xs = tuple(jnp.moveaxis(a, 2, 0) for a in (q_g, attn, u_v, w_k, k_end, g_end))
    S, o = lax.scan(step, S0, xs)
    o = jnp.moveaxis(o, 0, 2).reshape(B, H, n * c, DV)[:, :, :L]
    return jnp.moveaxis(o, 1, 2), S


def delta_mixer(h, conv_buf, S0, w_in, conv_w, a_log, dt_bias, onorm_w):
    B, L, _ = h.shape
    H = N_DELTA_HEADS
    proj = h @ w_in
    qkv, z, a, b, q_mem = jnp.split(proj, [QKV_W, QKV_W + TOK_W, QKV_W + TOK_W + H, QKV_W + TOK_W + 2 * H], axis=-1)
    qkv, conv_new = causal_conv(qkv, conv_buf, conv_w)
    q, k, v = (t.reshape(B, L, H, HEAD_DIM) for t in jnp.split(qkv, 3, axis=-1))
    q = l2norm(q) * HEAD_DIM ** -0.5
    k = l2norm(k)
    v = v.astype(jnp.float32)
    beta = jax.nn.sigmoid(b.astype(jnp.float32))
    g = -jnp.exp(a_log.astype(jnp.float32)) * jax.nn.softplus(a.astype(jnp.float32) + dt_bias.astype(jnp.float32))
    o, S = gated_delta_chunked(q, k, v, g, beta, S0.astype(jnp.float32))
    zf = z.reshape(B, L, H, HEAD_DIM).astype(jnp.float32)
    o = rmsnorm(o, onorm_w) * jax.nn.silu(zf)
    return o.reshape(B, L, TOK_W).astype(h.dtype), q_mem, conv_new, S.astype(S0.dtype)


def pool_mixer(h, buf, n_past, w_in, w_grp, scale):
    B, L, _ = h.shape
    proj = h @ w_in
    u, q_mem = proj[..., :TOK_W], proj[..., TOK_W:]
    up = jnp.concatenate([buf.astype(u.dtype), u], axis=1).astype(jnp.float32)
    csum = jnp.pad(jnp.cumsum(up, axis=1), ((0, 0), (1, 0), (0, 0)))
    end = csum[:, POOL_BUF + 1:]
    t = jnp.arange(L)
    outs = []
    for gi, w in enumerate(POOL_WINDOWS):
        sl = slice(gi * POOL_GROUP_W, (gi + 1) * POOL_GROUP_W)
        start = csum[:, POOL_BUF + 1 - w: POOL_BUF + 1 - w + L, sl]
        cnt = jnp.minimum(w, t + 1 + n_past).astype(jnp.float32)
        outs.append((end[..., sl] - start) / cnt[None, :, None])
    d = jnp.concatenate(outs, axis=-1) - up[:, POOL_BUF:]
    d = d.reshape(B, L, N_POOL_GROUPS, POOL_GROUP_W)
    y = jnp.einsum('blgc,gcd->blgd', d, w_grp.astype(jnp.float32)).reshape(B, L, TOK_W) * scale.astype(jnp.float32)
    return y.astype(h.dtype), q_mem, up[:, -POOL_BUF:].astype(buf.dtype)


def mem_kv(mem, norm_w, w_kv):
    B, M, _ = mem.shape
    k, v = jnp.split(rmsnorm(mem, norm_w) @ w_kv, 2, axis=-1)
    return k.reshape(B, M, N_MEM_HEADS, HEAD_DIM), v.reshape(B, M, N_MEM_HEADS, HEAD_DIM)


def cross_attn(q_mem, mk, mv):
    B, L, _ = q_mem.shape
    q = q_mem.reshape(B, L, N_MEM_HEADS, HEAD_DIM)
    s = jnp.einsum('blhd,bmhd->bhlm', q, mk.astype(q.dtype)).astype(jnp.float32) * HEAD_DIM ** -0.5
    p = jax.nn.softmax(s, axis=-1).astype(q.dtype)
    return jnp.einsum('bhlm,bmhd->blhd', p, mv.astype(q.dtype)).reshape(B, L, MEM_W)


def swiglu(h, w_gate_up, w_down):
    gate, up = jnp.split(h @ w_gate_up, 2, axis=-1)
    return (jax.nn.silu(gate) * up) @ w_down


def trunk(x, mem_k, mem_v, S_in, conv_in, pool_in, n_past, norm_mix, norm_ffn, norm_final,
          w_in_delta, conv_w, a_log, dt_bias, delta_onorm, w_in_pool, w_pool_grp, pool_scale,
          w_out, w_gate_up, w_down):
    S_out, conv_out, pool_out = [], [], []
    di = pi = 0
    for l in range(DEPTH):
        h = rmsnorm(x, norm_mix[l])
        if l % N_MIXERS == 0:
            y_tok, q_mem, conv_new, S_new = delta_mixer(h, conv_in[di], S_in[di], w_in_delta[di], conv_w[di],
                                                        a_log[di], dt_bias[di], delta_onorm[di])
            S_out.append(S_new)
            conv_out.append(conv_new)
            di += 1
        else:
            y_tok, q_mem, buf_new = pool_mixer(h, pool_in[pi], n_past, w_in_pool[pi], w_pool_grp[pi], pool_scale[pi])
            pool_out.append(buf_new)
            pi += 1
        y_mem = cross_attn(q_mem, mem_k[l], mem_v[l])
        x = x + jnp.concatenate([y_tok, y_mem], axis=-1) @ w_out[l]
        x = x + swiglu(rmsnorm(x, norm_ffn[l]), w_gate_up[l], w_down[l])
    return rmsnorm(x, norm_final), jnp.stack(S_out), jnp.stack(conv_out), jnp.stack(pool_out)


def setup_inputs(seed: int = 0) -> dict:
    key = jax.random.key(seed)
    ks = iter(jax.random.split(key, 40))
    nrm = lambda shape, scale: jax.random.normal(next(ks), shape, jnp.float32) * scale
    gain = lambda shape: 1.0 + 0.02 * jax.random.normal(next(ks), shape, jnp.float32)
    a_log = jnp.log(jax.random.uniform(next(ks), (N_DELTA, N_DELTA_HEADS), jnp.float32, 1.0, 16.0))
    dt = jnp.exp(jax.random.uniform(next(ks), (N_DELTA, N_DELTA_HEADS), jnp.float32, math.log(1e-3), math.log(1e-1)))
    dt_bias = dt + jnp.log(-jnp.expm1(-dt))
    return {
        'x_prompt': nrm((BATCH, SEQ, D_MODEL), 1.0),
        'x_sample': nrm((DEC_BATCH, DEC_SEQ, D_MODEL), 1.0),
        'mem_prompt': nrm((BATCH, N_MEM, D_MODEL), 1.0),
        'state_delta_S': nrm((N_DELTA, DEC_BATCH, N_DELTA_HEADS, HEAD_DIM, HEAD_DIM), HEAD_DIM ** -0.5),
        'state_delta_conv': nrm((N_DELTA, DEC_BATCH, CONV_W - 1, QKV_W), 1.0),
        'state_pool': nrm((N_POOL, DEC_BATCH, POOL_BUF, TOK_W), 1.0),
        'cache_mem_k': nrm((DEPTH, DEC_BATCH, N_MEM, N_MEM_HEADS, HEAD_DIM), 1.0),
        'cache_mem_v': nrm((DEPTH, DEC_BATCH, N_MEM, N_MEM_HEADS, HEAD_DIM), 1.0),
        'norm_mix': gain((DEPTH, D_MODEL)),
        'norm_ffn': gain((DEPTH, D_MODEL)),
        'norm_mem': gain((DEPTH, D_MODEL)),
        'norm_final': gain((D_MODEL,)),
        'w_in_delta': nrm((N_DELTA, D_MODEL, IN_DELTA), D_MODEL ** -0.5),
        'conv_w': nrm((N_DELTA, CONV_W, QKV_W), CONV_W ** -0.5),
        'a_log': a_log,
        'dt_bias': dt_bias,
        'delta_onorm': gain((N_DELTA, HEAD_DIM)),
        'w_in_pool': nrm((N_POOL, D_MODEL, IN_POOL), D_MODEL ** -0.5),
        'w_pool_grp': nrm((N_POOL, N_POOL_GROUPS, POOL_GROUP_W, POOL_GROUP_W), POOL_GROUP_W ** -0.5),
        'pool_scale': 1.0 + 0.1 * jax.random.normal(next(ks), (N_POOL, TOK_W), jnp.float32),
        'w_mem_kv': nrm((DEPTH, D_MODEL, 2 * MEM_W), D_MODEL ** -0.5),
        'w_out': nrm((DEPTH, MIX_W, D_MODEL), MIX_W ** -0.5),
        'w_gate_up': nrm((DEPTH, D_MODEL, 2 * D_FF), D_MODEL ** -0.5),
        'w_down': nrm((DEPTH, D_FF, D_MODEL), D_FF ** -0.5),
    }


def reference(x_prompt, x_sample, mem_prompt, state_delta_S, state_delta_conv, state_pool, cache_mem_k, cache_mem_v,
              norm_mix, norm_ffn, norm_mem, norm_final, w_in_delta, conv_w, a_log, dt_bias, delta_onorm,
              w_in_pool, w_pool_grp, pool_scale, w_mem_kv, w_out, w_gate_up, w_down):
    weights = (norm_mix, norm_ffn, norm_final, w_in_delta, conv_w, a_log, dt_bias, delta_onorm,
               w_in_pool, w_pool_grp, pool_scale, w_out, w_gate_up, w_down)
    B = x_prompt.shape[0]
    mkv = [mem_kv(mem_prompt, norm_mem[l], w_mem_kv[l]) for l in range(DEPTH)]
    p_mem_k = jnp.stack([kv[0] for kv in mkv])
    p_mem_v = jnp.stack([kv[1] for kv in mkv])
    S0 = jnp.zeros((N_DELTA, B, N_DELTA_HEADS, HEAD_DIM, HEAD_DIM), x_prompt.dtype)
    conv0 = jnp.zeros((N_DELTA, B, CONV_W - 1, QKV_W), x_prompt.dtype)
    pool0 = jnp.zeros((N_POOL, B, POOL_BUF, TOK_W), x_prompt.dtype)
    y_prompt, p_delta_S, p_delta_conv, p_pool = trunk(x_prompt, p_mem_k, p_mem_v, S0, conv0, pool0, 0, *weights)
    n_past = min(PAST_LEN, POOL_BUF)
    y_sample, s_delta_S, s_delta_conv, s_pool = trunk(x_sample, cache_mem_k, cache_mem_v, state_delta_S,
                                                      state_delta_conv, state_pool, n_past, *weights)
    return (y_prompt, y_sample, p_delta_S, p_delta_conv, p_pool, p_mem_k, p_mem_v, s_delta_S, s_delta_conv, s_pool)
```

```python
import numpy as np
from contextlib import ExitStack
import concourse.bass as bass
import concourse.mybir as mybir
from concourse.bass_utils import run_bass_kernel_spmd

F32 = mybir.dt.float32
BF16 = mybir.dt.bfloat16
AF = mybir.ActivationFunctionType
ALU = mybir.AluOpType

D = 2048
T = 1088
TP = 1024
NSQ = 16
TS = 64
TT = [(0, 384, 384), (384, 384, 384), (768, 320, 256)]
import os
DEPTH = 4
NPASS = 2
RUN_DEPTH = int(os.environ.get("K_DEPTH", "4"))
RUN_NPASS = int(os.environ.get("K_NPASS", "2"))
RUN_SKIP = os.environ.get("K_SKIP", "")
W_ND = (RUN_DEPTH + 1) // 2
W_NP = max(1, RUN_DEPTH // 2)
DFF = 5632
IN_DELTA = 6680
EPS = 1e-6
NBUF = 4
EPOCH = 12000
NQ = 4
FQ = 11

P_GMIX = 0
P_GFFN = 64
P_GMEM = 128
P_GFIN = 192
P_CONVW = 208
P_PSCALE = P_CONVW + 288
P_ONORM = P_PSCALE + 24
P_ALOG = P_ONORM + 2
P_DTB = P_ALOG + 24
P_CNT = P_DTB + 24
NP_ = P_CNT + 128
C_ID = 0
C_ONES = 128
C_AVG = 256
C_U = 384
C_MBSL = 512
C_MNUI = 640
C_US = 768
C_BMS = 896
C_MBSLS = 1024
C_MNUIS = 1152
C_SEL = 1280
C_EPS = C_SEL + 16
NC_ = C_EPS + 1


class Sched:
    ENGS = ('pe', 'act', 'dve', 'pool', 'sp')

    def __init__(self, nc):
        self.nc = nc
        self.streams = {e: [] for e in self.ENGS}
        self.cnt = {}
        self.epoch = {}
        self.lastw = {}
        self.readers = {}
        self.keys = []
        self.lastdma = {}

    def _newtok(self, base, inc):
        ep = self.epoch.get(base, 0)
        key = (base, ep)
        c = self.cnt.get(key, 0)
        if c + inc > EPOCH:
            ep += 1
            self.epoch[base] = ep
            key = (base, ep)
            c = 0
        if key not in self.cnt:
            self.keys.append(key)
        self.cnt[key] = c + inc
        return (key, c + inc)

    def inherit(self, new, olds):
        m = {}
        for o in olds:
            t = self.lastw.get(o)
            if t is not None and m.get(t[0], 0) < t[1]:
                m[t[0]] = t[1]
            for k, v in self.readers.get(o, {}).items():
                if m.get(k, 0) < v:
                    m[k] = v
        self.readers[new] = m
        self.lastw[new] = None

    def op(self, eng, fn, reads=(), writes=(), dma=None):
        deps = {}

        def add(t):
            if t is None:
                return
            k, v = t
            if deps.get(k, 0) < v:
                deps[k] = v
        for r in reads:
            add(self.lastw.get(r))
            if r.startswith('ps'):
                for k, v in self.readers.get(r, {}).items():
                    add((k, v))
        for w in writes:
            add(self.lastw.get(w))
            for k, v in self.readers.get(w, {}).items():
                add((k, v))
        if dma is None:
            tok = self._newtok(eng, 1)
        else:
            add(self.lastdma.get(dma))
            tok = self._newtok('dma_' + dma, 16)
            self.lastdma[dma] = tok
        self.streams[eng].append((deps, fn, tok, dma is not None))
        for r in reads:
            d = self.readers.setdefault(r, {})
            if d.get(tok[0], 0) < tok[1]:
                d[tok[0]] = tok[1]
        for w in writes:
            self.lastw[w] = tok
            self.readers[w] = {}
        return tok

    def emit(self):
        nc = self.nc
        final = [(k, self.cnt[k]) for k in self.keys]
        with ExitStack() as es:
            sems = {}
            for i, key in enumerate(self.keys):
                sems[key] = es.enter_context(nc.semaphore("s%d" % i))
            block = es.enter_context(nc.Block())

            def run(name, e):
                waited = {}
                for deps, fn, tok, isdma in self.streams[name]:
                    for k, v in deps.items():
                        if k[0] == 'pe' and name == 'pe':
                            continue
                        if waited.get(k, 0) >= v:
                            continue
                        e.wait_ge(sems[k], v)
                        waited[k] = v
                    ins = fn(e)
                    ins.then_inc(sems[tok[0]], 16 if isdma else 1)
                if name == 'sp':
                    for k, v in final:
                        e.wait_ge(sems[k], v)

            @block.tensor
            def _(e):
                run('pe', e)

            @block.scalar
            def _(e):
                run('act', e)

            @block.vector
            def _(e):
                run('dve', e)

            @block.gpsimd
            def _(e):
                run('pool', e)

            @block.sync
            def _(e):
                run('sp', e)


class Builder:
    def __init__(self, plan=None):
        self.plan = plan
        self.rec = []
        self.recording = plan is None

    def build(self):
        nc = bass.Bass("TRN2", target_bir_lowering=False)
        self.nc = nc
        S = Sched(nc)
        self.S = S
        dt = nc.dram_tensor

        def inp(name, shape):
            return dt(name, shape, F32, kind="ExternalInput").ap()

        def outp(name, shape):
            return dt(name, shape, F32, kind="ExternalOutput").ap()

        def scr(name, shape):
            return dt(name, shape, F32, kind="Internal").ap()
        self.d_x = inp("xT", [NPASS, D, T])
        self.d_mem = inp("memT", [D, 256])
        self.d_sS = inp("sS", [NPASS, 2, NSQ, 12, 128, 128])
        self.d_sconv = inp("sconvT", [NPASS, 2, 36, 128, NSQ * 3])
        self.d_spool = inp("spoolT", [NPASS, 2, 12, 128, NSQ * 15])
        self.d_ck = inp("ckT", [NPASS, DEPTH, NSQ, 128, 4 * 256])
        self.d_cv = inp("cv", [NPASS, DEPTH, NSQ, 256, 512])
        self.d_par = inp("params", [128, NP_])
        self.d_con = inp("consts", [128, NC_])
        self.w_in_delta = inp("w_in_delta", [W_ND, D, IN_DELTA])
        self.w_in_pool = inp("w_in_pool", [W_NP, D, D])
        self.w_pool_grp = inp("w_pool_grp", [W_NP, 4, 384, 384])
        self.w_mem_kv = inp("w_mem_kv", [RUN_DEPTH, D, 1024])
        self.w_out = inp("w_out", [RUN_DEPTH, D, D])
        self.w_gate_up = inp("w_gate_up", [RUN_DEPTH, D, 2 * DFF])
        self.w_down = inp("w_down", [RUN_DEPTH, DFF, D])

        self.o_y = outp("yT", [NPASS, D, T])
        self.o_pS = outp("o_pS", [2, 12, 128, 128])
        self.o_pconv = outp("o_pconvT", [2, 36, 128, 3])
        self.o_ppool = outp("o_ppoolT", [2, 12, 128, 15])
        self.o_mk = outp("o_mkT", [DEPTH, 128, 4 * 256])
        self.o_mv = outp("o_mv", [DEPTH, 256, 512])
        self.o_sS = outp("o_sS", [NPASS, 2, NSQ, 12, 128, 128])
        self.o_sconv = outp("o_sconvT", [NPASS, 2, 36, 128, NSQ * 3])
        self.o_spool = outp("o_spoolT", [NPASS, 2, 12, 128, NSQ * 15])
        self.dbg = os.environ.get("K_DBG", "0") == "1"
        if self.dbg:
            self.o_dbg = dt("dbg_y", [128, 16, T], BF16, kind="ExternalOutput").ap()
        self.c_S = scr("scr_S", [2, 12, 128, 128])
        self.c_conv = scr("scr_conv", [2, 36, 128, 3])
        self.c_pool = scr("scr_pool", [2, 12, 128, 15])

        with ExitStack() as es:
            sb = lambda name, shape, dtp: es.enter_context(nc.sbuf_tensor(name, shape, dtp))[:]
            self.xT = sb("xT_sb", [128, 16, T], F32)
            self.hT = sb("hT_sb", [128, 16, T], BF16)
            self.yb = sb("yb_sb", [128, 16, T], BF16)
            self.wsl = [sb("ws%d" % i, [128, 2048], BF16) for i in range(NBUF)]
            self.par = sb("par_sb", [128, NP_], F32)
            self.con = sb("con_sb", [128, NC_], F32)
            self.onesb = sb("onesb", [128, 128], BF16)
            self.kTb = sb("kTb", [128, 4, 256], BF16)
            self.vb = sb("vb", [128, 2, 512], BF16)
            self.AR = (nc.sbuf_bytes_remaining - int(os.environ.get("K_RES", "4096"))) // 4
            self.arena = sb("arena", [128, self.AR], F32)
            self.ps = [es.enter_context(nc.psum_tensor("ps%d" % i, [128, 512], F32))[:] for i in range(8)]
            self.psi = 0
            self.ws_i = 0
            self.ws_issued = 0
            self.ws_rel = 0
            self.ar_off = 0
            self.ar_names = []
            self.ar_prev = []
            self.dmarr = 0
            self.program()
            if not self.recording:
                S.emit()
        return nc

    def bank(self):
        i = self.psi % 8
        self.psi += 1
        return self.ps[i], 'ps%d' % i

    def phase(self):
        S = self.S
        m = dict(getattr(self, 'fence', {}))
        for o in self.ar_names:
            t = S.lastw.get(o)
            if t is not None and m.get(t[0], 0) < t[1]:
                m[t[0]] = t[1]
            for k, v in S.readers.get(o, {}).items():
                if m.get(k, 0) < v:
                    m[k] = v
        self.fence = m
        self.ar_names = []
        self.ar_off = 0

    def alloc(self, name, n):
        assert self.ar_off + n <= self.AR, ("arena overflow", name, self.ar_off + n, self.AR)
        ap = self.arena[:, self.ar_off:self.ar_off + n]
        self.ar_off += n
        self._uid = getattr(self, '_uid', 0) + 1
        rn = "%s#%d" % (name, self._uid)
        self.S.readers[rn] = dict(getattr(self, 'fence', {}))
        self.S.lastw[rn] = None
        self.ar_names.append(rn)
        return ap, rn

    def rsqrt(self, out_ap, out_r, pin, pin_r, scale=1.0):
        S = self.S
        S.op('act', lambda e: e.activation(out=out_ap, in_=pin, func=AF.Ln, bias=self.con[:, C_EPS:C_EPS + 1], scale=scale),
             reads=[pin_r, 'con'], writes=[out_r])
        S.op('act', lambda e: e.activation(out=out_ap, in_=out_ap, func=AF.Exp, scale=-0.5), reads=[out_r], writes=[out_r])

    def dq(self):
        self.dmarr += 1
        return 'q%d' % (self.dmarr % 16)

    def C(self, off, n=128, rows=128):
        return self.con[0:rows, off:off + n]

    def Pc(self, off, n=1):
        return self.par[:, off:off + n]

    def panel(self, src, nk, ncols):
        idx = self.ws_i
        self.ws_i += 1
        sl = idx % NBUF
        if self.recording:
            self.rec.append((src, nk, ncols))
        else:
            assert idx < self.ws_rel + NBUF, "too many held panels"
            self._ws_issue()
            assert self.ws_issued > idx
        return self.wsl[sl][:, 0:nk * ncols].rearrange("p (k m) -> p k m", m=ncols), 'ws%d' % sl

    def _ws_issue(self):
        S = self.S
        while self.ws_issued < min(len(self.plan), self.ws_rel + NBUF):
            j = self.ws_issued
            psrc, pnk, pnc = self.plan[j]
            sl = j % NBUF
            dst = self.wsl[sl][:, 0:pnk * pnc].rearrange("p (k m) -> p k m", m=pnc)
            S.op('pool', lambda e, dst=dst, psrc=psrc: e.dma_start(out=dst, in_=psrc),
                 writes=['ws%d' % sl], dma='ws%d' % sl)
            self.ws_issued += 1

    def release(self, n=1):
        self.ws_rel = getattr(self, 'ws_rel', 0) + n
        if not self.recording:
            self._ws_issue()

    def wview(self, w2d, c0, ncols, k0=0, nk=16):
        return w2d[k0 * 128:(k0 + nk) * 128, c0:c0 + ncols].rearrange("(k p) m -> p k m", p=128)

    def mm_group(self, out_ap, pairs, reads, wres):
        n = len(pairs)

        def fn(e):
            ins = None
            for i, (l, r) in enumerate(pairs):
                ins = e.matmul(out_ap, lhsT=l, rhs=r, start=(i == 0), stop=(i == n - 1))
            return ins
        return self.S.op('pe', fn, reads=reads, writes=[wres])

    def rmsnorm_to_h(self, gcol):
        S = self.S
        self.phase()
        sq, sq_r = self.alloc("nsq", T)
        rs, rs_r = self.alloc("nrs", T)
        xr = lambda c, ti: 'x%d_%d' % (c, ti)
        banks = [self.bank() for _ in TT]
        for c in range(16):
            S.op('act', lambda e, c=c: e.activation(out=sq, in_=self.xT[:, c, :], func=AF.Square),
                 reads=[xr(c, ti) for ti in range(3)], writes=[sq_r])
            for ti, (t0, tn, _) in enumerate(TT):
                pb, pr = banks[ti]
                S.op('pe', lambda e, pb=pb, t0=t0, tn=tn, c=c: e.matmul(
                    pb[:, 0:tn], lhsT=self.C(C_AVG), rhs=sq[:, t0:t0 + tn], start=(c == 0), stop=(c == 15)),
                    reads=[sq_r, 'con'], writes=[pr])
        for ti, (t0, tn, _) in enumerate(TT):
            pb, pr = banks[ti]
            self.rsqrt(rs[:, t0:t0 + tn], rs_r, pb[:, 0:tn], pr)
        for c in range(16):
            S.op('dve', lambda e, c=c: e.scalar_tensor_tensor(
                out=self.hT[:, c, :], in0=self.xT[:, c, :], scalar=self.Pc(gcol + c), in1=rs,
                op0=ALU.mult, op1=ALU.mult),
                reads=[xr(c, ti) for ti in range(3)] + [rs_r, 'par'], writes=['h%d' % c])

    def gemm_x_update(self, wsrc_fn, nk, rhs_fn, rhs_reads):
        S = self.S
        for m in range(16):
            wp, wr = self.panel(wsrc_fn(m), nk, 128)
            for ti, (t0, tn, _) in enumerate(TT):
                pb, pr = self.bank()
                self.mm_group(pb[:, 0:tn], [(wp[:, k, :], rhs_fn(k, t0, tn)) for k in range(nk)],
                              reads=[wr] + rhs_reads, wres=pr)
                xres = 'x%d_%d' % (m, ti)
                S.op('dve', lambda e, pb=pb, m=m, t0=t0, tn=tn: e.tensor_tensor(
                    out=self.xT[:, m, t0:t0 + tn], in0=pb[:, 0:tn], in1=self.xT[:, m, t0:t0 + tn], op=ALU.add),
                    reads=[pr, xres], writes=[xres])
            self.release(1)

    def ffn(self, l):
        S = self.S
        hreads = ['h%d' % c for c in range(16)]
        for q in range(NQ):
            for f in range(FQ):
                fc = q * FQ + f
                wg, wgr = self.panel(self.wview(self.w_gate_up[l], fc * 128, 128), 16, 128)
                wu, wur = self.panel(self.wview(self.w_gate_up[l], DFF + fc * 128, 128), 16, 128)
                for ti, (t0, tn, _) in enumerate(TT):
                    pg, pgr = self.bank()
                    pu, pur = self.bank()
                    self.mm_group(pg[:, 0:tn], [(wg[:, k, :], self.hT[:, k, t0:t0 + tn]) for k in range(16)],
                                  reads=[wgr] + hreads, wres=pgr)
                    self.mm_group(pu[:, 0:tn], [(wu[:, k, :], self.hT[:, k, t0:t0 + tn]) for k in range(16)],
                                  reads=[wur] + hreads, wres=pur)
                    sg, sgr = self.sgbuf[(fc * 3 + ti) % 2]
                    S.op('act', lambda e, pg=pg, sg=sg, tn=tn: e.activation(out=sg[:, 0:tn], in_=pg[:, 0:tn], func=AF.Silu),
                         reads=[pgr], writes=[sgr])
                    S.op('dve', lambda e, pu=pu, sg=sg, f=f, t0=t0, tn=tn: e.tensor_tensor(
                        out=self.yb[:, f, t0:t0 + tn], in0=pu[:, 0:tn], in1=sg[:, 0:tn], op=ALU.mult),
                        reads=[pur, sgr], writes=['y%d' % f])
                self.release(2)
            self.gemm_x_update(lambda m, q=q: self.wview(self.w_down[l], m * 128, 128, k0=q * FQ, nk=FQ), FQ,
                               lambda k, t0, tn: self.yb[:, k, t0:t0 + tn], ['y%d' % f for f in range(FQ)])

    def mem_kv(self, l, write_out):
        S = self.S
        self.phase()
        mh, mh_r = self.alloc("memhat", 16 * 128)
        mhb = mh.bitcast(BF16).rearrange("p (c m) -> p c m", m=256)
        mf, mf_r = self.alloc("memf", 16 * 128)
        mfv = mf.rearrange("p (c m) -> p c m", m=128)
        sq, sq_r = self.alloc("memsq", 128)
        rs, rs_r = self.alloc("memrs", 128)
        ko, ko_r = self.alloc("memko", 1024)
        vo, vo_r = self.alloc("memvo", 1024)
        for mt0 in range(2):
            S.op('sp', lambda e, mt0=mt0: e.dma_start(
                out=mfv, in_=self.d_mem[:, mt0 * 128:(mt0 + 1) * 128].rearrange("(c p) m -> p c m", p=128)),
                writes=[mf_r], dma=self.dq())
            pb, pr = self.bank()
            for c in range(16):
                S.op('act', lambda e, c=c: e.activation(out=sq, in_=mfv[:, c, :], func=AF.Square), reads=[mf_r], writes=[sq_r])
                S.op('pe', lambda e, c=c, pb=pb: e.matmul(pb[:, 0:128], lhsT=self.C(C_AVG), rhs=sq, start=(c == 0), stop=(c == 15)),
                     reads=[sq_r, 'con'], writes=[pr])
            self.rsqrt(rs, rs_r, pb[:, 0:128], pr)
            for c in range(16):
                S.op('dve', lambda e, c=c, mt0=mt0: e.scalar_tensor_tensor(
                    out=mhb[:, c, mt0 * 128:(mt0 + 1) * 128], in0=mfv[:, c, :],
                    scalar=self.Pc(P_GMEM + l * 16 + c), in1=rs, op0=ALU.mult, op1=ALU.mult),
                    reads=[mf_r, rs_r, 'par'], writes=[mh_r])
        kov = ko.rearrange("p (h m) -> p h m", m=256)
        if 'mkvout' in RUN_SKIP:
            write_out = False
        for hm in range(4 if 'mkvK' not in RUN_SKIP else 0):
            wp, wr = self.panel(self.wview(self.w_mem_kv[l], hm * 128, 128), 16, 128)
            pb, pr = self.bank()
            self.mm_group(pb[:, 0:256], [(wp[:, k, :], mhb[:, k, :]) for k in range(16)], reads=[wr, mh_r], wres=pr)
            S.op('act', lambda e, pb=pb, hm=hm: e.copy(out=self.kTb[:, hm, :], in_=pb[:, 0:256]), reads=[pr], writes=['kTb'])
            if write_out:
                S.op('dve', lambda e, pb=pb, hm=hm: e.tensor_copy(out=kov[:, hm, :], in_=pb[:, 0:256]), reads=[pr], writes=[ko_r])
            self.release(1)
        if 'mkvV' in RUN_SKIP:
            return
        wv = [self.panel(self.wview(self.w_mem_kv[l], 512 + j * 128, 128), 16, 128) for j in range(4)]
        vov = vo.rearrange("p (t c) -> p t c", c=512)
        for mt in range(2):
            pb, pr = self.bank()
            for j in range(4):
                wp, wr = wv[j]
                self.mm_group(pb[:, j * 128:(j + 1) * 128],
                              [(mhb[:, k, mt * 128:(mt + 1) * 128], wp[:, k, :]) for k in range(16)],
                              reads=[wr, mh_r], wres=pr)
            S.op('act', lambda e, pb=pb, mt=mt: e.copy(out=self.vb[:, mt, :], in_=pb[:, 0:512]), reads=[pr], writes=['vb'])
            if write_out:
                S.op('dve', lambda e, pb=pb, mt=mt: e.tensor_copy(out=vov[:, mt, :], in_=pb[:, 0:512]), reads=[pr], writes=[vo_r])
        self.release(4)
        if write_out and 'mko' not in RUN_SKIP:
            S.op('sp', lambda e: e.dma_start(out=self.o_mk[l], in_=ko), reads=[ko_r], dma=self.dq())
        if write_out and 'mvo' not in RUN_SKIP:
            S.op('sp', lambda e: e.dma_start(out=self.o_mv[l].rearrange("(t p) c -> p t c", p=128), in_=vov),
                 reads=[vo_r], dma=self.dq())

    def xattn_prompt_tile(self, qm, qm_r, hm, ti):
        S = self.S
        t0, tn, tnp = TT[ti]
        e_aps = []
        for mt in range(2):
            pb, pr = self.bank()
            self.mm_group(pb[:, 0:tnp], [(self.kTb[:, hm, mt * 128:(mt + 1) * 128], qm[:, 0:tnp])], reads=['kTb', qm_r], wres=pr)
            eb, eb_r = self.ebuf[mt]
            S.op('act', lambda e, pb=pb, eb=eb, tnp=tnp: e.activation(out=eb[:, 0:tnp], in_=pb[:, 0:tnp], func=AF.Exp,
                                                                     scale=128.0 ** -0.5), reads=[pr], writes=[eb_r])
            e_aps.append((eb, eb_r))
        po, por = self.bank()
        pd, pdr = self.bank()
        self.mm_group(po[:, 0:tnp], [(self.vb[:, mt, hm * 128:(hm + 1) * 128], e_aps[mt][0][:, 0:tnp]) for mt in range(2)],
                      reads=['vb', e_aps[0][1], e_aps[1][1]], wres=por)
        self.mm_group(pd[:, 0:tnp], [(self.onesb[:, :], e_aps[mt][0][:, 0:tnp]) for mt in range(2)],
                      reads=['onesb', e_aps[0][1], e_aps[1][1]], wres=pdr)
        rd, rd_r = self.rdbuf
        S.op('dve', lambda e, pd=pd, tnp=tnp: e.reciprocal(out=rd[:, 0:tnp], in_=pd[:, 0:tnp]), reads=[pdr], writes=[rd_r])
        S.op('dve', lambda e, po=po, t0=t0, tnp=tnp, hm=hm: e.tensor_tensor(
            out=self.yb[:, 12 + hm, t0:t0 + tnp], in0=po[:, 0:tnp], in1=rd[:, 0:tnp], op=ALU.mult),
            reads=[por, rd_r], writes=['y%d' % (12 + hm)])

    def qmem_and_prompt_attn(self, wsrc_fn):
        S = self.S
        qs, qs_r = self.alloc("qs", 4 * TS)
        qsv = qs.rearrange("p (h t) -> p h t", t=TS)
        qm, qm_r = self.alloc("qm", 192)
        qmb = qm.bitcast(BF16)
        e0, e0r = self.alloc("e0", 192)
        e1, e1r = self.alloc("e1", 192)
        self.ebuf = [(e0.bitcast(BF16), e0r), (e1.bitcast(BF16), e1r)]
        self.rdbuf = self.alloc("rd", 384)
        hreads = ['h%d' % c for c in range(16)]
        for hm in range(4):
            wp, wr = self.panel(wsrc_fn(hm), 16, 128)
            for ti, (t0, tn, tnp) in enumerate(TT):
                pb, pr = self.bank()
                self.mm_group(pb[:, 0:tn], [(wp[:, k, :], self.hT[:, k, t0:t0 + tn]) for k in range(16)],
                              reads=[wr] + hreads, wres=pr)
                S.op('act', lambda e, pb=pb, tnp=tnp: e.copy(out=qmb[:, 0:tnp], in_=pb[:, 0:tnp]), reads=[pr], writes=[qm_r])
                if ti == 2:
                    S.op('dve', lambda e, pb=pb, hm=hm: e.tensor_copy(out=qsv[:, hm, :], in_=pb[:, 256:320]),
                         reads=[pr], writes=[qs_r])
                self.xattn_prompt_tile(qmb, qm_r, hm, ti)
            self.release(1)
        return qsv, qs_r

    def xattn_sample(self, l, ps_, qsv, qs_r):
        S = self.S
        kt = [self.alloc("skt%d" % i, 1024) for i in range(2)]
        vt = [self.alloc("svt%d" % i, 1024) for i in range(2)]
        ee = [self.alloc("see%d" % i, 32) for i in range(2)]
        rd = [self.alloc("srd%d" % i, 16) for i in range(2)]
        for s in range(NSQ):
            k_ap, k_r = kt[s % 2]
            v_ap, v_r = vt[s % 2]
            e_ap, e_r = ee[s % 2]
            r_ap, r_r = rd[s % 2]
            kv = k_ap.rearrange("p (h m) -> p h m", m=256)
            vv = v_ap.rearrange("p (t c) -> p t c", c=512)
            ev = e_ap.rearrange("p (h t q) -> p h t q", t=2, q=4)
            S.op('sp', lambda e, k_ap=k_ap, s=s: e.dma_start(out=k_ap, in_=self.d_ck[ps_, l, s]), writes=[k_r], dma=self.dq())
            S.op('sp', lambda e, vv=vv, s=s: e.dma_start(out=vv, in_=self.d_cv[ps_, l, s].rearrange("(t p) c -> p t c", p=128)),
                 writes=[v_r], dma=self.dq())
            pb, pr = self.bank()

            def sc(e, pb=pb, kv=kv, s=s):
                ins = None
                for hm in range(4):
                    for mt in range(2):
                        o = (hm * 2 + mt) * 4
                        ins = e.matmul(pb[:, o:o + 4], lhsT=kv[:, hm, mt * 128:(mt + 1) * 128],
                                       rhs=qsv[:, hm, s * 4:s * 4 + 4], start=True, stop=True)
                return ins
            S.op('pe', sc, reads=[k_r, qs_r], writes=[pr])
            S.op('act', lambda e, pb=pb, e_ap=e_ap: e.activation(out=e_ap, in_=pb[:, 0:32], func=AF.Exp, scale=128.0 ** -0.5),
                 reads=[pr], writes=[e_r])
            po, por = self.bank()

            def pv(e, po=po, vv=vv, ev=ev):
                ins = None
                for hm in range(4):
                    for mt in range(2):
                        ins = e.matmul(po[:, hm * 4:hm * 4 + 4], lhsT=vv[:, mt, hm * 128:(hm + 1) * 128],
                                       rhs=ev[:, hm, mt, :], start=(mt == 0), stop=(mt == 1))
                for mt in range(2):
                    ins = e.matmul(po[:, 16:32].rearrange("p (h q) -> p h q", q=4), lhsT=self.C(C_ONES),
                                   rhs=ev[:, :, mt, :], start=(mt == 0), stop=(mt == 1))
                return ins
            S.op('pe', pv, reads=[v_r, e_r, 'con'], writes=[por])
            S.op('dve', lambda e, po=po, r_ap=r_ap: e.reciprocal(out=r_ap, in_=po[:, 16:32]), reads=[por], writes=[r_r])
            S.op('dve', lambda e, po=po, r_ap=r_ap, s=s: e.tensor_tensor(
                out=self.yb[:, 12:16, TP + s * 4:TP + s * 4 + 4], in0=po[:, 0:16].rearrange("p (h q) -> p h q", q=4),
                in1=r_ap.rearrange("p (h q) -> p h q", q=4), op=ALU.mult),
                reads=[por, r_r], writes=['y12', 'y13', 'y14', 'y15'])

    def pool_layer(self, l, ps_):
        S = self.S
        pi = l // 2
        self.phase()
        hreads = ['h%d' % c for c in range(16)]
        WP = 15 + TP + NSQ * 19
        up = [self.alloc("up%d" % i, WP) for i in range(2)]
        w1, w1_r = self.alloc("pw1", WP)
        w2, w2_r = self.alloc("pw2", WP)
        dbf = [self.alloc("pd%d" % i, T // 2) for i in range(3)]
        t16, t16_r = self.alloc("pt16", 16)
        wins = [2, 4, 8, 16]
        for g in range(4):
            for j in range(3):
                c = g * 3 + j
                u_ap, u_r = up[(g * 3 + j) % 2]
                usv = u_ap[:, 15 + TP:WP].rearrange("p (s r) -> p s r", r=19)
                if ps_ == 0:
                    S.op('dve', lambda e, u_ap=u_ap: e.memset(u_ap[:, 0:15], 0.0), writes=[u_r])
                else:
                    S.op('sp', lambda e, u_ap=u_ap, c=c: e.dma_start(out=u_ap[:, 0:15], in_=self.c_pool[pi, c]),
                         reads=['c_pool%d_%d' % (pi, c)], writes=[u_r], dma=self.dq())
                S.op('sp', lambda e, usv=usv, c=c: e.dma_start(
                    out=usv[:, :, 0:15], in_=self.d_spool[ps_, pi, c].rearrange("p (s r) -> p s r", r=15)),
                    writes=[u_r], dma=self.dq())
                wp, wr = self.panel(self.wview(self.w_in_pool[pi], c * 128, 128), 16, 128)
                for ti, (t0, tn, tnp) in enumerate(TT):
                    pb, pr = self.bank()
                    self.mm_group(pb[:, 0:tn], [(wp[:, k, :], self.hT[:, k, t0:t0 + tn]) for k in range(16)],
                                  reads=[wr] + hreads, wres=pr)
                    S.op('act', lambda e, pb=pb, u_ap=u_ap, t0=t0, tnp=tnp: e.copy(out=u_ap[:, 15 + t0:15 + t0 + tnp], in_=pb[:, 0:tnp]),
                         reads=[pr], writes=[u_r])
                    if ti == 2:
                        S.op('act', lambda e, pb=pb, usv=usv: e.copy(out=usv[:, :, 15:19],
                                                                      in_=pb[:, 256:320].rearrange("p (s q) -> p s q", q=4)),
                             reads=[pr], writes=[u_r])
                self.release(1)
                S.op('sp', lambda e, usv=usv, c=c: e.dma_start(
                    out=self.o_spool[ps_, pi, c].rearrange("p (s r) -> p s r", r=15), in_=usv[:, :, 4:19]),
                    reads=[u_r], dma=self.dq())
                if ps_ == 0:
                    S.op('sp', lambda e, u_ap=u_ap, c=c: e.dma_start(out=self.c_pool[pi, c], in_=u_ap[:, TP:TP + 15]),
                         reads=[u_r], writes=['c_pool%d_%d' % (pi, c)], dma=self.dq())
                else:
                    S.op('sp', lambda e, u_ap=u_ap, c=c: e.dma_start(out=self.o_ppool[pi, c], in_=u_ap[:, TP:TP + 15]),
                         reads=[u_r], dma=self.dq())
                src, src_r = u_ap, u_r
                sh = 1
                k = 0
                while sh < wins[g]:
                    dst, dst_r = (w1, w1_r) if k % 2 == 0 else (w2, w2_r)
                    eng = 'dve'
                    S.op(eng, lambda e, dst=dst, src=src, sh=sh: e.tensor_tensor(
                        out=dst[:, sh:WP], in0=src[:, sh:WP], in1=src[:, 0:WP - sh], op=ALU.add),
                        reads=[src_r], writes=[dst_r])
                    src, src_r = dst, dst_r
                    sh *= 2
                    k += 1
                d_ap, d_r = dbf[j]
                db = d_ap.bitcast(BF16)
                iw = 1.0 / wins[g]
                S.op('dve', lambda e, db=db, src=src, u_ap=u_ap, iw=iw: e.scalar_tensor_tensor(
                    out=db[:, 0:TP], in0=src[:, 15:15 + TP], scalar=iw, in1=u_ap[:, 15:15 + TP],
                    op0=ALU.mult, op1=ALU.subtract), reads=[src_r, u_r], writes=[d_r])
                ssv = src[:, 15 + TP:WP].rearrange("p (s r) -> p s r", r=19)
                S.op('dve', lambda e, db=db, ssv=ssv, usv=usv, iw=iw: e.scalar_tensor_tensor(
                    out=db[:, TP:T].rearrange("p (s q) -> p s q", q=4), in0=ssv[:, :, 15:19], scalar=iw, in1=usv[:, :, 15:19],
                    op0=ALU.mult, op1=ALU.subtract), reads=[src_r, u_r], writes=[d_r])
                cnt = self.Pc(P_CNT + ps_ * 64 + g * 16, 16)
                S.op('dve', lambda e, src=src, cnt=cnt: e.tensor_tensor(out=t16, in0=src[:, 15:31], in1=cnt, op=ALU.mult),
                     reads=[src_r, 'par'], writes=[t16_r])
                S.op('dve', lambda e, db=db, u_ap=u_ap: e.tensor_tensor(out=db[:, 0:16], in0=t16, in1=u_ap[:, 15:31], op=ALU.subtract),
                     reads=[t16_r, u_r], writes=[d_r])
            for mo in range(3):
                wp, wr = self.panel(self.w_pool_grp[pi, g][:, mo * 128:(mo + 1) * 128].rearrange("(k p) m -> p k m", p=128), 3, 128)
                co = g * 3 + mo
                for ti, (t0, tn, tnp) in enumerate(TT):
                    pb, pr = self.bank()
                    self.mm_group(pb[:, 0:tn], [(wp[:, k, :], dbf[k][0].bitcast(BF16)[:, t0:t0 + tn]) for k in range(3)],
                                  reads=[wr] + [dbf[k][1] for k in range(3)], wres=pr)
                    S.op('act', lambda e, pb=pb, co=co, t0=t0, tn=tn: e.activation(
                        out=self.yb[:, co, t0:t0 + tn], in_=pb[:, 0:tn], func=AF.Identity, scale=self.Pc(P_PSCALE + pi * 12 + co)),
                        reads=[pr, 'par'], writes=['y%d' % co])
                self.release(1)
        self.phase()
        qsv, qs_r = self.qmem_and_prompt_attn(lambda hm: self.wview(self.w_in_pool[pi], 1536 + hm * 128, 128))
        self.xattn_sample(l, ps_, qsv, qs_r)

    def delta_layer(self, l, ps_):
        S = self.S
        di = l // 2
        self.phase()
        hreads = ['h%d' % c for c in range(16)]
        W = self.w_in_delta[di]
        A = lambda name, n: self.alloc(name, n)
        NCH = 9
        tg = {}
        for nm in ('g', 'beta', 'gc', 'ngam', 'ks', 'gend', 'tmp'):
            ap, r = A("tg_" + nm, NCH * 12)
            tg[nm] = (ap.rearrange("p (c h) -> p c h", h=12), r)
        wab, wabr = self.panel(self.wview(W, 6144, 24), 16, 24)
        pab, pabr = self.bank()
        pabv = pab[:, 0:NCH * 24].rearrange("p (c j) -> p c j", j=24)
        for ci in range(NCH):
            tok0 = ci * 128
            ntok = 128 if ci < 8 else 64
            self.mm_group(pabv[0:ntok, ci, :], [(self.hT[:, k, tok0:tok0 + ntok], wab[:, k, :]) for k in range(16)],
                          reads=[wabr] + hreads, wres=pabr)
        self.release(1)
        gv, g_r = tg['g']
        bv, b_r = tg['beta']
        tv, t_r = tg['tmp']
        alog = self.par[:, P_ALOG + di * 12:P_ALOG + di * 12 + 12]
        dtb = self.par[:, P_DTB + di * 12:P_DTB + di * 12 + 12]
        nA, nA_r = A("negA", 12)
        S.op('act', lambda e: e.activation(out=nA, in_=alog, func=AF.Exp), reads=['par'], writes=[nA_r])
        S.op('dve', lambda e: e.memset(gv, 0.0), writes=[g_r])
        S.op('dve', lambda e: e.memset(bv, 0.0), writes=[b_r])
        S.op('dve', lambda e: e.memset(tv, 0.0), writes=[t_r])
        for ci in range(NCH):
            nt = 128 if ci < 8 else 64
            S.op('dve', lambda e, ci=ci, nt=nt: e.tensor_tensor(out=tv[0:nt, ci, :], in0=pabv[0:nt, ci, 0:12], in1=dtb[0:nt, :], op=ALU.add),
                 reads=[pabr, 'par'], writes=[t_r])
            S.op('act', lambda e, ci=ci, nt=nt: e.activation(out=bv[0:nt, ci, :], in_=pabv[0:nt, ci, 12:24], func=AF.Sigmoid),
                 reads=[pabr], writes=[b_r])
        S.op('act', lambda e: e.activation(out=tv, in_=tv, func=AF.Exp), reads=[t_r], writes=[t_r])
        S.op('act', lambda e: e.activation(out=tv, in_=tv, func=AF.Ln, bias=1.0), reads=[t_r], writes=[t_r])
        for ci in range(NCH):
            S.op('dve', lambda e, ci=ci: e.scalar_tensor_tensor(out=gv[:, ci, :], in0=tv[:, ci, :], scalar=-1.0, in1=nA,
                                                                op0=ALU.mult, op1=ALU.mult), reads=[t_r, nA_r], writes=[g_r])
        S.op('dve', lambda e: e.memset(gv[64:128, 8, :], 0.0), writes=[g_r])
        gcv, gc_r = tg['gc']
        ngv, ng_r = tg['ngam']
        ksv, ks_r = tg['ks']
        gev, ge_r = tg['gend']
        pg1, pg1r = self.bank()
        pg2, pg2r = self.bank()
        p1v = pg1[:, 0:NCH * 12].rearrange("p (c h) -> p c h", h=12)
        p2v = pg2[:, 0:NCH * 12].rearrange("p (c h) -> p c h", h=12)
        for ci in range(NCH):
            Uc = self.C(C_U) if ci < 8 else self.C(C_US)
            Oc = self.C(C_ONES) if ci < 8 else self.C(C_BMS)
            S.op('pe', lambda e, ci=ci, Uc=Uc: e.matmul(p1v[:, ci, :], lhsT=Uc, rhs=gv[:, ci, :], start=True, stop=True),
                 reads=[g_r, 'con'], writes=[pg1r])
            S.op('pe', lambda e, ci=ci, Oc=Oc: e.matmul(p2v[:, ci, :], lhsT=Oc, rhs=gv[:, ci, :], start=True, stop=True),
                 reads=[g_r, 'con'], writes=[pg2r])
        S.op('dve', lambda e: e.tensor_copy(out=gcv, in_=p1v), reads=[pg1r], writes=[gc_r])
        S.op('act', lambda e: e.activation(out=ngv, in_=p1v, func=AF.Exp), reads=[pg1r], writes=[ng_r])
        S.op('dve', lambda e: e.tensor_scalar(out=ngv, in0=ngv, scalar1=-1.0, scalar2=None, op0=ALU.mult), reads=[ng_r], writes=[ng_r])
        S.op('act', lambda e: e.activation(out=gev, in_=p2v, func=AF.Exp), reads=[pg2r], writes=[ge_r])
        S.op('dve', lambda e: e.tensor_tensor(out=ksv, in0=p2v, in1=gcv, op=ALU.subtract), reads=[pg2r, gc_r], writes=[ks_r])
        S.op('act', lambda e: e.activation(out=ksv, in_=ksv, func=AF.Exp), reads=[ks_r], writes=[ks_r])

        pads = [A("pad%d" % i, 387) for i in range(3)]
        spads = [A("spad%d" % i, NSQ * 7) for i in range(3)]
        cb = [A("cb%d" % i, 384) for i in range(3)]
        zs, zs_r = A("zs", 384)
        ob, ob_r = A("ob", 384)
        t1, t1_r = A("t1", 384)
        t1s = [(t1, t1_r), A("t1k", 384)]
        CT = []
        for i in range(3):
            d_ = {}
            for nm in ('A0', 'A1', 'B0', 'B1', 'X0', 'X1', 'aT', 'XB', 'qg'):
                d_[nm] = A("c%d_%s" % (i, nm), 128)
            CT.append(d_)
        MT = {}
        for nm in ('Gam', 'kend', 'vtok', 'R', 'u'):
            MT[nm] = A("m_" + nm, 128)
        Sb = [A("S%d" % i, 128) for i in range(2)]
        GQ = 1
        sinB = [A("sin%d" % i, GQ * 128) for i in range(2)]
        soutB = [A("sout%d" % i, GQ * 128) for i in range(2)]
        umB = [A("um%d" % i, GQ * 128) for i in range(2)]
        gsel = A("gsel", 16)
        gends = A("gends", 16)
        qsv = None
        ID = self.C(C_ID)
        evk = [0]

        def evac(out, pin, reads, wres):
            evk[0] += 1
            if evk[0] % 2:
                return S.op('act', lambda e: e.copy(out=out, in_=pin), reads=reads, writes=[wres])
            return S.op('dve', lambda e: e.tensor_copy(out=out, in_=pin), reads=reads, writes=[wres])

        def bankfn(i):
            st = [0]

            def f():
                b = 2 * i + (st[0] % 2)
                st[0] += 1
                return self.ps[b], 'ps%d' % b
            return f

        def prep_gen(hd, ci, n, c0, sample, T_, bk):
            qn, qn_r = cb[0]
            kn, kn_r = cb[1]
            q_ = qn[:, c0:c0 + n]
            k_ = kn[:, c0:c0 + n]
            Um = self.C(C_U, n, n) if not sample else self.C(C_US, n, n)
            MB = self.C(C_MBSL, n, n) if not sample else self.C(C_MBSLS, n, n)
            MN = self.C(C_MNUI, n, n) if not sample else self.C(C_MNUIS, n, n)
            gcol = gcv[0:n, ci, hd:hd + 1]
            bcol = bv[0:n, ci, hd:hd + 1]
            tl = lambda nm: (T_[nm][0][0:n, 0:n], T_[nm][1])
            GU, GU_r = tl('A1')
            S.op('dve', lambda e: e.tensor_scalar(out=GU, in0=Um, scalar1=gv[0:n, ci, hd:hd + 1], scalar2=None, op0=ALU.mult),
                 reads=['con', g_r], writes=[GU_r])
            yield
            pG2, pG2r = bk()
            S.op('pe', lambda e: e.matmul(pG2[:, 0:n], lhsT=self.C(C_ONES, 128, n), rhs=GU, start=True, stop=True),
                 reads=[GU_r, 'con'], writes=[pG2r])
            yield
            E1, E1_r = tl('B1')
            E2, E2_r = tl('X1')
            S.op('dve', lambda e: e.scalar_tensor_tensor(out=E1, in0=pG2[0:n, 0:n], scalar=gcol, in1=MN, op0=ALU.subtract, op1=ALU.min),
                 reads=[pG2r, gc_r, 'con'], writes=[E1_r])
            yield
            S.op('dve', lambda e: e.scalar_tensor_tensor(out=E2, in0=pG2[0:n, 0:n], scalar=gcol, in1=MB, op0=ALU.subtract, op1=ALU.max),
                 reads=[pG2r, gc_r, 'con'], writes=[E2_r])
            yield
            qg, qg_r = (T_['qg'][0][:, 0:n], T_['qg'][1])
            if sample:
                Gam, Gam_r = (MT['Gam'][0][:, 0:n], MT['Gam'][1])
            else:
                Gam, Gam_r = qg, qg_r
            S.op('act', lambda e: e.activation(out=Gam, in_=pG2[:, 0:n], func=AF.Exp), reads=[pG2r], writes=[Gam_r])
            yield
            S.op('dve', lambda e: e.tensor_tensor(out=qg, in0=q_, in1=Gam, op=ALU.mult), reads=[qn_r, Gam_r], writes=[qg_r])
            yield
            S.op('act', lambda e: e.activation(out=E1, in_=E1, func=AF.Exp), reads=[E1_r], writes=[E1_r])
            yield
            S.op('act', lambda e: e.activation(out=E2, in_=E2, func=AF.Exp, scale=-1.0), reads=[E2_r], writes=[E2_r])
            yield
            pK, pKr = bk()
            S.op('pe', lambda e: e.matmul(pK[0:n, 0:n], lhsT=k_, rhs=k_, start=True, stop=True), reads=[kn_r], writes=[pKr])
            yield
            A0, A0_r = tl('A0')
            S.op('dve', lambda e: e.scalar_tensor_tensor(out=A0, in0=pK[0:n, 0:n], scalar=bcol, in1=E2,
                                                         op0=ALU.mult, op1=ALU.mult), reads=[pKr, b_r, E2_r], writes=[A0_r])
            yield
            pQ, pQr = bk()
            S.op('pe', lambda e: e.matmul(pQ[0:n, 0:n], lhsT=k_, rhs=q_, start=True, stop=True), reads=[kn_r, qn_r], writes=[pQr])
            yield
            aT, aT_r = tl('aT')
            S.op('dve', lambda e: e.tensor_tensor(out=aT, in0=pQ[0:n, 0:n], in1=E1, op=ALU.mult), reads=[pQr, E1_r], writes=[aT_r])
            yield
            pT, pTr = bk()
            S.op('pe', lambda e: e.matmul(pT[0:n, 0:n], lhsT=A0, rhs=ID[0:n, 0:n], start=True, stop=True), reads=[A0_r, 'con'], writes=[pTr])
            yield
            B0, B0_r = tl('B0')
            evac(B0, pT[0:n, 0:n], [pTr], B0_r)
            yield
            X0, X0_r = tl('X0')
            S.op('dve', lambda e: e.scalar_tensor_tensor(out=X0, in0=B0, scalar=-1.0, in1=ID[0:n, 0:n], op0=ALU.mult, op1=ALU.add),
                 reads=[B0_r, 'con'], writes=[X0_r])
            yield
            nlev = 1 if sample else 6
            Ac, Ac_r = A0, A0_r
            Bc, Bc_r = B0, B0_r
            Xc, Xc_r = X0, X0_r
            for lev in range(nlev):
                An, An_r = tl('A1') if lev % 2 == 0 else tl('A0')
                Bn, Bn_r = tl('B1') if lev % 2 == 0 else tl('B0')
                Xn, Xn_r = tl('X1') if lev % 2 == 0 else tl('X0')
                last = (lev == nlev - 1)
                pa, par_ = bk()
                S.op('pe', lambda e, pa=pa, Bc=Bc, Ac=Ac: e.matmul(pa[0:n, 0:n], lhsT=Bc, rhs=Ac, start=True, stop=True),
                     reads=[Ac_r, Bc_r], writes=[par_])
                yield
                if not last:
                    pb_, pbr_ = bk()
                    S.op('pe', lambda e, pb_=pb_, Bc=Bc, Ac=Ac: e.matmul(pb_[0:n, 0:n], lhsT=Ac, rhs=Bc, start=True, stop=True),
                         reads=[Ac_r, Bc_r], writes=[pbr_])
                    yield
                evac(An, pa[0:n, 0:n], [par_], An_r)
                yield
                if not last:
                    evac(Bn, pb_[0:n, 0:n], [pbr_], Bn_r)
                    yield
                px, pxr = bk()
                S.op('pe', lambda e, px=px, An=An, Xc=Xc: e.matmul(px[0:n, 0:n], lhsT=An, rhs=Xc, start=True, stop=True),
                     reads=[An_r, Xc_r], writes=[pxr])
                yield
                S.op('dve', lambda e, px=px, Xc=Xc, Xn=Xn: e.tensor_tensor(out=Xn, in0=px[0:n, 0:n], in1=Xc, op=ALU.add),
                     reads=[pxr, Xc_r], writes=[Xn_r])
                yield
                Ac, Ac_r, Bc, Bc_r, Xc, Xc_r = An, An_r, Bn, Bn_r, Xn, Xn_r
            XB, XB_r = tl('XB')
            S.op('dve', lambda e: e.tensor_scalar(out=XB, in0=Xc, scalar1=bcol, scalar2=None, op0=ALU.mult),
                 reads=[Xc_r, b_r], writes=[XB_r])
            yield
            vc, vc_r = cb[2]
            v_ = vc[:, c0:c0 + n]
            pk, pkr = bk()
            S.op('pe', lambda e: e.matmul(pk[0:n, 0:128], lhsT=k_, rhs=ID, start=True, stop=True), reads=[kn_r, 'con'], writes=[pkr])
            yield
            kend, kend_r = (T_['B0'][0][0:n, :], T_['B0'][1])
            S.op('dve', lambda e: e.tensor_scalar(out=kend, in0=pk[0:n, 0:128], scalar1=ksv[0:n, ci, hd:hd + 1], scalar2=None, op0=ALU.mult),
                 reads=[pkr, ks_r], writes=[kend_r])
            yield
            if not sample:
                pv_, pvr = bk()
                S.op('pe', lambda e: e.matmul(pv_[0:n, 0:128], lhsT=v_, rhs=ID, start=True, stop=True), reads=[vc_r, 'con'], writes=[pvr])
                yield
                vtok, vtok_r = (T_['B1'][0][0:n, :], T_['B1'][1])
                evac(vtok, pv_[0:n, 0:128], [pvr], vtok_r)
                yield
                kntok, kntok_r = (T_['X0'][0][0:n, :], T_['X0'][1])
                evac(kntok, pk[0:n, 0:128], [pkr], kntok_r)
                yield
                pU, pUr = bk()
                S.op('pe', lambda e: e.matmul(pU[0:n, 0:128], lhsT=XB, rhs=vtok, start=True, stop=True), reads=[XB_r, vtok_r], writes=[pUr])
                yield
                U0, U0_r = (T_['A0'][0][0:n, :], T_['A0'][1])
                evac(U0, pU[0:n, 0:128], [pUr], U0_r)
                yield
                XBg, XBg_r = tl('X1')
                S.op('dve', lambda e: e.tensor_scalar(out=XBg, in0=XB, scalar1=ngv[0:n, ci, hd:hd + 1], scalar2=None, op0=ALU.mult),
                     reads=[XB_r, ng_r], writes=[XBg_r])
                yield
                pW, pWr = bk()
                S.op('pe', lambda e: e.matmul(pW[0:n, 0:128], lhsT=XBg, rhs=kntok, start=True, stop=True), reads=[XBg_r, kntok_r], writes=[pWr])
                yield
                Wn, Wn_r = (T_['A1'][0][0:n, :], T_['A1'][1])
                evac(Wn, pW[0:n, 0:128], [pWr], Wn_r)
                yield
                pP, pPr = bk()
                S.op('pe', lambda e: e.matmul(pP[:, 0:128], lhsT=Wn, rhs=kend, start=True, stop=True), reads=[Wn_r, kend_r], writes=[pPr])
                yield
                TT, TT_r = (T_['X0'][0][:, :], T_['X0'][1])
                S.op('dve', lambda e: e.scalar_tensor_tensor(out=TT, in0=ID, scalar=gev[:, ci, hd:hd + 1], in1=pP[:, 0:128],
                                                             op0=ALU.mult, op1=ALU.add), reads=['con', ge_r, pPr], writes=[TT_r])
                yield
                pQ2, pQ2r = bk()
                S.op('pe', lambda e: e.matmul(pQ2[:, 0:n], lhsT=Wn, rhs=aT, start=True, stop=True), reads=[Wn_r, aT_r], writes=[pQ2r])
                yield
                S.op('dve', lambda e: e.tensor_tensor(out=qg, in0=pQ2[:, 0:n], in1=qg, op=ALU.add), reads=[pQ2r, qg_r], writes=[qg_r])
                yield

        def spath(hd, ci, n, c0, Sc, Sn, sample, T_):
            kn, kn_r = cb[1]
            vc, vc_r = cb[2]
            k_ = kn[:, c0:c0 + n]
            v_ = vc[:, c0:c0 + n]
            tl = lambda nm: (T_[nm][0][0:n, 0:n], T_[nm][1])
            aT, aT_r = tl('aT')
            XB, XB_r = tl('XB')
            qg, qg_r = (T_['qg'][0][:, 0:n], T_['qg'][1])
            Gam, Gam_r = (MT['Gam'][0][:, 0:n], MT['Gam'][1])
            kend, kend_r = (T_['B0'][0][0:n, :], T_['B0'][1])
            R, R_r = (MT['R'][0][0:n, :], MT['R'][1])
            u, u_r = (MT['u'][0][0:n, :], MT['u'][1])
            ngcol = ngv[0:n, ci, hd:hd + 1]
            if not sample:
                Sc_ap, Sc_r = Sc
                Sn_ap, Sn_r = Sn
                U0, U0_r = (T_['A0'][0][0:n, :], T_['A0'][1])
                TT, TT_r = (T_['X0'][0][:, :], T_['X0'][1])
                pS, pSr = self.bank()
                self.mm_group(pS[:, 0:128], [(TT, Sc_ap), (kend, U0)], reads=[TT_r, Sc_r, kend_r, U0_r], wres=pSr)
                evac(Sn_ap, pS[:, 0:128], [pSr], Sn_r)
                po, por = self.bank()
                self.mm_group(po[:, 0:n], [(Sc_ap, qg), (U0, aT)], reads=[Sc_r, qg_r, U0_r, aT_r], wres=por)
                evac(ob[:, c0:c0 + n], po[:, 0:n], [por], ob_r)
            else:
                sel = self.con[0:64, C_SEL:C_SEL + 16]
                gs_ap, gs_r = gsel
                gd_ap, gd_r = gends
                S.op('dve', lambda e: e.tensor_scalar(out=gs_ap[0:64, :], in0=sel, scalar1=gv[0:64, ci, hd:hd + 1], scalar2=None, op0=ALU.mult),
                     reads=['con', g_r], writes=[gs_r])
                pge, pger = self.bank()
                S.op('pe', lambda e: e.matmul(pge[:, 0:16], lhsT=self.C(C_ONES, 128, 64), rhs=gs_ap[0:64, :], start=True, stop=True),
                     reads=[gs_r, 'con'], writes=[pger])
                S.op('act', lambda e: e.activation(out=gd_ap, in_=pge[:, 0:16], func=AF.Exp), reads=[pger], writes=[gd_r])
                p1, p1r = self.bank()
                po, por = self.bank()
                def bufs(gq):
                    sin_ap, sin_r = sinB[gq % 2]
                    sout_ap, sout_r = soutB[gq % 2]
                    um_ap, um_r = umB[gq % 2]
                    return (sin_ap.rearrange("p (s v) -> p s v", v=128), sin_r,
                            sout_ap.rearrange("p (s v) -> p s v", v=128), sout_r,
                            um_ap[0:64, :].rearrange("p (s v) -> p s v", v=128), um_r)
                for gq in range(NSQ // GQ):
                    sinv, sin_r, soutv, sout_r, umv, um_r = bufs(gq)
                    S.op('sp', lambda e, gq=gq, sinv=sinv: e.dma_start(
                        out=sinv, in_=self.d_sS[ps_, di, gq * GQ:gq * GQ + GQ, hd].rearrange("s k v -> k s v")),
                        writes=[sin_r], dma=self.dq())

                    def f1(e, gq=gq, sinv=sinv):
                        ins = None
                        for s4 in range(GQ):
                            s = gq * GQ + s4
                            ins = e.matmul(p1[:, s * 4:s * 4 + 4], lhsT=sinv[:, s4, :], rhs=k_[:, s * 4:s * 4 + 4], start=True, stop=True)
                            ins = e.matmul(po[:, s * 4:s * 4 + 4], lhsT=sinv[:, s4, :], rhs=qg[:, s * 4:s * 4 + 4], start=True, stop=True)
                        return ins
                    S.op('pe', f1, reads=[sin_r, kn_r, qg_r], writes=[p1r, por])
                RT, RT_r = (T_['A0'][0][:, 0:n], T_['A0'][1])
                S.op('dve', lambda e: e.tensor_tensor(out=RT, in0=p1[:, 0:n], in1=Gam, op=ALU.mult), reads=[p1r, Gam_r], writes=[RT_r])
                S.op('dve', lambda e: e.tensor_tensor(out=RT, in0=v_, in1=RT, op=ALU.subtract), reads=[vc_r, RT_r], writes=[RT_r])
                pr_, prr = self.bank()
                S.op('pe', lambda e: e.matmul(pr_[0:n, 0:128], lhsT=RT, rhs=ID, start=True, stop=True), reads=[RT_r, 'con'], writes=[prr])
                evac(R, pr_[0:n, 0:128], [prr], R_r)
                pu_, pur = self.bank()
                S.op('pe', lambda e: e.matmul(pu_[0:n, 0:128], lhsT=XB, rhs=R, start=True, stop=True), reads=[XB_r, R_r], writes=[pur])
                evac(u, pu_[0:n, 0:128], [pur], u_r)
                evac(ob[:, c0:c0 + n], po[:, 0:n], [por], ob_r)
                po2, po2r = self.bank()
                S.op('pe', lambda e: e.matmul(po2[:, 0:n], lhsT=u, rhs=aT, start=True, stop=True),
                     reads=[u_r, aT_r], writes=[po2r])
                S.op('dve', lambda e: e.tensor_tensor(out=ob[:, c0:c0 + n], in0=po2[:, 0:n], in1=ob[:, c0:c0 + n], op=ALU.add),
                     reads=[po2r, ob_r], writes=[ob_r])
                for gq in range(NSQ // GQ):
                    sinv, sin_r, soutv, sout_r, umv, um_r = bufs(gq)
                    S.op('sp', lambda e, gq=gq, sinv=sinv: e.dma_start(
                        out=sinv, in_=self.d_sS[ps_, di, gq * GQ:gq * GQ + GQ, hd].rearrange("s k v -> k s v")),
                        writes=[sin_r], dma=self.dq())
                    for s4 in range(GQ):
                        s = gq * GQ + s4
                        S.op('dve', lambda e, s=s, s4=s4, umv=umv: e.tensor_scalar(out=umv[:, s4, :], in0=u, scalar1=sel[:, s:s + 1], scalar2=None, op0=ALU.mult),
                             reads=[u_r, 'con'], writes=[um_r])
                    for s4 in range(GQ):
                        s = gq * GQ + s4
                        pS, pSr = self.bank()
                        S.op('pe', lambda e, pS=pS, s4=s4, umv=umv: e.matmul(pS[:, 0:128], lhsT=kend, rhs=umv[:, s4, :], start=True, stop=True),
                             reads=[kend_r, um_r], writes=[pSr])
                        S.op('dve', lambda e, pS=pS, s=s, s4=s4, soutv=soutv, sinv=sinv: e.scalar_tensor_tensor(
                            out=soutv[:, s4, :], in0=sinv[:, s4, :], scalar=gd_ap[:, s:s + 1], in1=pS[:, 0:128], op0=ALU.mult, op1=ALU.add),
                            reads=[sin_r, gd_r, pSr], writes=[sout_r])
                    S.op('sp', lambda e, gq=gq, soutv=soutv: e.dma_start(
                        out=self.o_sS[ps_, di, gq * GQ:gq * GQ + GQ, hd].rearrange("s k v -> k s v"), in_=soutv),
                        reads=[sout_r], dma=self.dq())

        for hd in range(12):
            wq = [self.panel(self.wview(W, j * 1536 + hd * 128, 128), 16, 128) for j in range(3)]
            wz = self.panel(self.wview(W, 4608 + hd * 128, 128), 16, 128)
            cidx = [hd, 12 + hd, 24 + hd]
            for j in range(3):
                p_ap, p_r = pads[j]
                s_ap, s_r = spads[j]
                sv = s_ap.rearrange("p (s r) -> p s r", r=7)
                if ps_ == 0:
                    S.op('dve', lambda e, p_ap=p_ap: e.memset(p_ap[:, 0:3], 0.0), writes=[p_r])
                else:
                    S.op('sp', lambda e, cidx=cidx, p_ap=p_ap, j=j: e.dma_start(out=p_ap[:, 0:3], in_=self.c_conv[di, cidx[j]]),
                         reads=['c_conv%d_%d' % (di, cidx[j])], writes=[p_r], dma=self.dq())
                S.op('sp', lambda e, cidx=cidx, sv=sv, j=j: e.dma_start(
                    out=sv[:, :, 0:3], in_=self.d_sconv[ps_, di, cidx[j]].rearrange("p (s r) -> p s r", r=3)),
                    writes=[s_r], dma=self.dq())
            S0, S0_r = Sb[0]
            if ps_ == 0:
                S.op('dve', lambda e: e.memset(S0, 0.0), writes=[S0_r])
            else:
                S.op('sp', lambda e, hd=hd: e.dma_start(out=S0, in_=self.c_S[di, hd]), reads=['c_S%d_%d' % (di, hd)],
                     writes=[S0_r], dma=self.dq())
            scur = 0
            for ti, (t0, tn, tnp) in enumerate(TT):
                def stream_gen(j, bk, ti=ti, t0=t0, tn=tn, tnp=tnp):
                    t1, t1_r = t1s[min(j, 1)]
                    wp, wr = wq[j]
                    p_ap, p_r = pads[j]
                    s_ap, s_r = spads[j]
                    sv = s_ap.rearrange("p (s r) -> p s r", r=7)
                    c_ap, c_r = cb[j]
                    pb, pr = bk()
                    self.mm_group(pb[:, 0:tn], [(wp[:, k, :], self.hT[:, k, t0:t0 + tn]) for k in range(16)],
                                  reads=[wr] + hreads, wres=pr)
                    yield
                    S.op('act', lambda e, pb=pb, p_ap=p_ap, tnp=tnp: e.copy(out=p_ap[:, 3:3 + tnp], in_=pb[:, 0:tnp]),
                         reads=[pr], writes=[p_r])
                    yield
                    cw = lambda tap, cidx=cidx, j=j: self.Pc(P_CONVW + (di * 36 + cidx[j]) * 4 + tap)
                    S.op('dve', lambda e, c_ap=c_ap, p_ap=p_ap, tnp=tnp, cw=cw: e.tensor_scalar(
                        out=c_ap[:, 0:tnp], in0=p_ap[:, 3:3 + tnp], scalar1=cw(3), scalar2=None, op0=ALU.mult),
                        reads=[p_r, 'par'], writes=[c_r])
                    yield
                    for tap in range(3):
                        S.op('dve', lambda e, c_ap=c_ap, p_ap=p_ap, tnp=tnp, cw=cw, tap=tap: e.scalar_tensor_tensor(
                            out=c_ap[:, 0:tnp], in0=p_ap[:, tap:tap + tnp], scalar=cw(tap), in1=c_ap[:, 0:tnp],
                            op0=ALU.mult, op1=ALU.add), reads=[p_r, c_r, 'par'], writes=[c_r])
                        yield
                    if ti == 2:
                        cs = c_ap[:, 256:320].rearrange("p (s q) -> p s q", q=4)
                        S.op('act', lambda e, pb=pb, sv=sv: e.copy(out=sv[:, :, 3:7], in_=pb[:, 256:320].rearrange("p (s q) -> p s q", q=4)),
                             reads=[pr], writes=[s_r])
                        yield
                        S.op('dve', lambda e, cs=cs, sv=sv, cw=cw: e.tensor_scalar(
                            out=cs, in0=sv[:, :, 3:7], scalar1=cw(3), scalar2=None, op0=ALU.mult), reads=[s_r, 'par'], writes=[c_r])
                        yield
                        for tap in range(3):
                            S.op('dve', lambda e, cs=cs, sv=sv, cw=cw, tap=tap: e.scalar_tensor_tensor(
                                out=cs, in0=sv[:, :, tap:tap + 4], scalar=cw(tap), in1=cs, op0=ALU.mult, op1=ALU.add),
                                reads=[s_r, c_r, 'par'], writes=[c_r])
                            yield
                        S.op('sp', lambda e, cidx=cidx, sv=sv, j=j: e.dma_start(
                            out=self.o_sconv[ps_, di, cidx[j]].rearrange("p (s r) -> p s r", r=3), in_=sv[:, :, 4:7]),
                            reads=[s_r], dma=self.dq())
                        if ps_ == 0:
                            S.op('sp', lambda e, cidx=cidx, p_ap=p_ap, j=j, tnp=tnp: e.dma_start(out=self.c_conv[di, cidx[j]], in_=p_ap[:, tnp:tnp + 3]),
                                 reads=[p_r], writes=['c_conv%d_%d' % (di, cidx[j])], dma=self.dq())
                        else:
                            S.op('sp', lambda e, cidx=cidx, p_ap=p_ap, j=j, tnp=tnp: e.dma_start(out=self.o_pconv[di, cidx[j]], in_=p_ap[:, tnp:tnp + 3]),
                                 reads=[p_r], dma=self.dq())
                    else:
                        S.op('act', lambda e, p_ap=p_ap, tnp=tnp: e.copy(out=p_ap[:, 0:3], in_=p_ap[:, tnp:tnp + 3]),
                             reads=[p_r], writes=[p_r])
                    yield
                    tb, tb_r = (t1, t1_r) if j < 2 else (ob, ob_r)
                    S.op('act', lambda e, c_ap=c_ap, tn=tn, tb=tb: e.activation(out=tb[:, 0:tn], in_=c_ap[:, 0:tn], func=AF.Exp, scale=-1.0),
                         reads=[c_r], writes=[tb_r])
                    yield
                    S.op('act', lambda e, tn=tn, tb=tb: e.activation(out=tb[:, 0:tn], in_=tb[:, 0:tn], func=AF.Ln, bias=1.0),
                         reads=[tb_r], writes=[tb_r])
                    yield
                    S.op('act', lambda e, tn=tn, tb=tb: e.activation(out=tb[:, 0:tn], in_=tb[:, 0:tn], func=AF.Exp, scale=-1.0),
                         reads=[tb_r], writes=[tb_r])
                    yield
                    S.op('dve', lambda e, c_ap=c_ap, tn=tn, tb=tb: e.tensor_tensor(out=c_ap[:, 0:tn], in0=c_ap[:, 0:tn], in1=tb[:, 0:tn], op=ALU.mult),
                         reads=[c_r, tb_r], writes=[c_r])
                    yield
                    if j < 2:
                        S.op('act', lambda e, c_ap=c_ap, tn=tn: e.activation(out=t1[:, 0:tn], in_=c_ap[:, 0:tn], func=AF.Square),
                             reads=[c_r], writes=[t1_r])
                        yield
                        pn, pnr = bk()
                        S.op('pe', lambda e, pn=pn, tn=tn: e.matmul(pn[:, 0:tn], lhsT=self.C(C_ONES), rhs=t1[:, 0:tn], start=True, stop=True),
                             reads=[t1_r, 'con'], writes=[pnr])
                        yield
                        self.rsqrt(t1[:, 0:tn], t1_r, pn[:, 0:tn], pnr)
                        yield
                        sc_ = (128.0 ** -0.5) if j == 0 else 1.0
                        S.op('dve', lambda e, c_ap=c_ap, tn=tn, sc_=sc_: e.scalar_tensor_tensor(
                            out=c_ap[:, 0:tn], in0=c_ap[:, 0:tn], scalar=sc_, in1=t1[:, 0:tn], op0=ALU.mult, op1=ALU.mult),
                            reads=[c_r, t1_r], writes=[c_r])
                        yield
                def z_gen(bk, tn=tn, t0=t0):
                  wp, wr = wz
                  pb, pr = bk()
                  if True:
                    self.mm_group(pb[:, 0:tn], [(wp[:, k, :], self.hT[:, k, t0:t0 + tn]) for k in range(16)], reads=[wr] + hreads, wres=pr)
                    yield
                    S.op('act', lambda e, pb=pb, tn=tn: e.activation(out=zs[:, 0:tn], in_=pb[:, 0:tn], func=AF.Exp, scale=-1.0), reads=[pr], writes=[zs_r])
                    yield
                    S.op('act', lambda e, tn=tn: e.activation(out=zs[:, 0:tn], in_=zs[:, 0:tn], func=AF.Ln, bias=1.0), reads=[zs_r], writes=[zs_r])
                    yield
                    S.op('act', lambda e, tn=tn: e.activation(out=zs[:, 0:tn], in_=zs[:, 0:tn], func=AF.Exp, scale=-1.0), reads=[zs_r], writes=[zs_r])
                    yield
                    S.op('dve', lambda e, pb=pb, tn=tn: e.tensor_tensor(out=zs[:, 0:tn], in0=pb[:, 0:tn], in1=zs[:, 0:tn], op=ALU.mult),
                         reads=[pr, zs_r], writes=[zs_r])
                    yield
                gens2 = [stream_gen(j, bankfn(j)) for j in range(3)] + [z_gen(bankfn(3))]
                while gens2:
                    for g_ in list(gens2):
                        try:
                            next(g_)
                        except StopIteration:
                            gens2.remove(g_)
                if ti == 2:
                    self.release(4)
                nch = 3 if ti < 2 else 2
                gens = [prep_gen(hd, ti * 3 + cj, 128, cj * 128, False, CT[cj], bankfn(cj)) for cj in range(nch)]
                if ti == 2:
                    gens.append(prep_gen(hd, 8, 64, 256, True, CT[2], bankfn(2)))
                if 'prep' in RUN_SKIP:
                    gens = []
                while gens:
                    for g_ in list(gens):
                        try:
                            next(g_)
                        except StopIteration:
                            gens.remove(g_)
                for cj in range(nch if 'spath' not in RUN_SKIP else 0):
                    spath(hd, ti * 3 + cj, 128, cj * 128, Sb[scur], Sb[1 - scur], False, CT[cj])
                    scur = 1 - scur
                if ti == 2 and 'spath' not in RUN_SKIP:
                    spath(hd, 8, 64, 256, None, None, True, CT[2])
                S.op('act', lambda e, tn=tn: e.activation(out=t1[:, 0:tn], in_=ob[:, 0:tn], func=AF.Square), reads=[ob_r], writes=[t1_r])
                pn, pnr = self.bank()
                S.op('pe', lambda e, pn=pn, tn=tn: e.matmul(pn[:, 0:tn], lhsT=self.C(C_ONES), rhs=t1[:, 0:tn], start=True, stop=True),
                     reads=[t1_r, 'con'], writes=[pnr])
                self.rsqrt(t1[:, 0:tn], t1_r, pn[:, 0:tn], pnr, scale=1.0 / 128.0)
                S.op('dve', lambda e, tn=tn: e.scalar_tensor_tensor(out=ob[:, 0:tn], in0=ob[:, 0:tn], scalar=self.Pc(P_ONORM + di),
                                                                    in1=t1[:, 0:tn], op0=ALU.mult, op1=ALU.mult),
                     reads=[ob_r, t1_r, 'par'], writes=[ob_r])
                S.op('dve', lambda e, tn=tn, t0=t0, hd=hd: e.tensor_tensor(out=self.yb[:, hd, t0:t0 + tn], in0=ob[:, 0:tn], in1=zs[:, 0:tn], op=ALU.mult),
                     reads=[ob_r, zs_r], writes=['y%d' % hd])
            Sf, Sf_r = Sb[scur]
            if ps_ == 0:
                S.op('sp', lambda e, Sf=Sf, hd=hd: e.dma_start(out=self.c_S[di, hd], in_=Sf), reads=[Sf_r],
                     writes=['c_S%d_%d' % (di, hd)], dma=self.dq())
            else:
                S.op('sp', lambda e, Sf=Sf, hd=hd: e.dma_start(out=self.o_pS[di, hd], in_=Sf), reads=[Sf_r], dma=self.dq())
        self.phase()
        qsv, qs_r = self.qmem_and_prompt_attn(lambda hm: self.wview(W, 6168 + hm * 128, 128))
        if 'sattn' not in RUN_SKIP:
            self.xattn_sample(l, ps_, qsv, qs_r)

    def program(self):
        S = self.S
        S.op('sp', lambda e: e.dma_start(out=self.par, in_=self.d_par), writes=['par'], dma=self.dq())
        S.op('sp', lambda e: e.dma_start(out=self.con, in_=self.d_con), writes=['con'], dma=self.dq())
        S.op('dve', lambda e: e.memset(self.onesb, 1.0), writes=['onesb'])
        for ps_ in range(RUN_NPASS):
            for c in range(16):
                S.op('sp', lambda e, c=c, ps_=ps_: e.dma_start(out=self.xT[:, c, :], in_=self.d_x[ps_, c * 128:(c + 1) * 128, :]),
                     writes=['x%d_%d' % (c, ti) for ti in range(3)], dma=self.dq())
            for l in range(RUN_DEPTH):
                if 'memkv' not in RUN_SKIP:
                    self.mem_kv(l, write_out=(ps_ == 0))
                self.rmsnorm_to_h(P_GMIX + l * 16)
                if 'mixer' not in RUN_SKIP:
                    if l % 2 == 0:
                        self.delta_layer(l, ps_)
                    else:
                        self.pool_layer(l, ps_)
                if self.dbg and l == RUN_DEPTH - 1 and ps_ == 0:
                    S.op('sp', lambda e: e.dma_start(out=self.o_dbg, in_=self.yb), reads=['y%d' % c for c in range(16)], dma=self.dq())
                if 'wout' not in RUN_SKIP:
                    self.gemm_x_update(lambda m, l=l: self.wview(self.w_out[l], m * 128, 128), 16,
                                       lambda k, t0, tn: self.yb[:, k, t0:t0 + tn], ['y%d' % c for c in range(16)])
                self.rmsnorm_to_h(P_GFFN + l * 16)
                self.phase()
                self.sgbuf = [self.alloc("sg%d" % i, 384) for i in range(2)]
                if 'ffn' not in RUN_SKIP:
                    self.ffn(l)
            self.final_norm_out(ps_)

    def final_norm_out(self, ps_):
        S = self.S
        self.phase()
        sq, sq_r = self.alloc("fsq", T)
        rs, rs_r = self.alloc("frs", T)
        st = [self.alloc("fst%d" % i, T) for i in range(2)]
        xr = lambda c, ti: 'x%d_%d' % (c, ti)
        banks = [self.bank() for _ in TT]
        for c in range(16):
            S.op('act', lambda e, c=c: e.activation(out=sq, in_=self.xT[:, c, :], func=AF.Square),
                 reads=[xr(c, ti) for ti in range(3)], writes=[sq_r])
            for ti, (t0, tn, _) in enumerate(TT):
                pb, pr = banks[ti]
                S.op('pe', lambda e, pb=pb, t0=t0, tn=tn, c=c: e.matmul(
                    pb[:, 0:tn], lhsT=self.C(C_AVG), rhs=sq[:, t0:t0 + tn], start=(c == 0), stop=(c == 15)),
                    reads=[sq_r, 'con'], writes=[pr])
        for ti, (t0, tn, _) in enumerate(TT):
            pb, pr = banks[ti]
            self.rsqrt(rs[:, t0:t0 + tn], rs_r, pb[:, 0:tn], pr)
        for c in range(16):
            s_ap, s_r = st[c % 2]
            S.op('dve', lambda e, c=c, s_ap=s_ap: e.scalar_tensor_tensor(
                out=s_ap, in0=self.xT[:, c, :], scalar=self.Pc(P_GFIN + c), in1=rs, op0=ALU.mult, op1=ALU.mult),
                reads=[xr(c, ti) for ti in range(3)] + [rs_r, 'par'], writes=[s_r])
            S.op('sp', lambda e, c=c, s_ap=s_ap: e.dma_start(out=self.o_y[ps_, c * 128:(c + 1) * 128, :], in_=s_ap),
                 reads=[s_r], dma=self.dq())


_CACHE = {}


def build_nc():
    if 'nc' in _CACHE:
        return _CACHE['nc']
    b0 = Builder(plan=None)
    b0.build()
    b1 = Builder(plan=b0.rec)
    nc = b1.build()
    assert b1.ws_i == len(b0.rec)
    _CACHE['nc'] = nc
    return nc


def make_consts():
    c = np.zeros((128, NC_), np.float32)
    i = np.arange(128)
    c[:, C_ID:C_ID + 128] = np.eye(128, dtype=np.float32)
    c[:, C_ONES:C_ONES + 128] = 1.0
    c[:, C_AVG:C_AVG + 128] = 1.0 / 2048.0
    U = (i[:, None] <= i[None, :]).astype(np.float32)
    c[:, C_U:C_U + 128] = U
    SL = (i[:, None] > i[None, :])
    c[:, C_MBSL:C_MBSL + 128] = np.where(SL, 0.0, 1e30)
    UI = (i[None, :] >= i[:, None])
    c[:, C_MNUI:C_MNUI + 128] = np.where(UI, 0.0, -1e30)
    seq = i // 4
    same = (seq[:, None] == seq[None, :])
    c[:, C_US:C_US + 128] = (same & (i[:, None] <= i[None, :])).astype(np.float32)
    c[:, C_BMS:C_BMS + 128] = same.astype(np.float32)
    c[:, C_MBSLS:C_MBSLS + 128] = np.where(same & SL, 0.0, 1e30)
    c[:, C_MNUIS:C_MNUIS + 128] = np.where(same & UI, 0.0, -1e30)
    sel = np.zeros((128, 16), np.float32)
    sel[np.arange(64), np.arange(64) // 4] = 1.0
    c[:, C_SEL:C_SEL + 16] = sel
    c[:, C_EPS] = EPS
    return c


def make_params(inp):
    p = np.zeros((128, NP_), np.float32)
    fm = lambda a: np.ascontiguousarray(a.reshape(a.shape[0], -1, 128).transpose(2, 0, 1)).reshape(128, -1)
    p[:, P_GMIX:P_GMIX + 64] = fm(inp['norm_mix'])
    p[:, P_GFFN:P_GFFN + 64] = fm(inp['norm_ffn'])
    p[:, P_GMEM:P_GMEM + 64] = fm(inp['norm_mem'])
    p[:, P_GFIN:P_GFIN + 16] = fm(inp['norm_final'][None, :])
    cw = inp['conv_w']
    p[:, P_CONVW:P_CONVW + 288] = np.ascontiguousarray(cw.reshape(2, 4, 36, 128).transpose(3, 0, 2, 1)).reshape(128, 288)
    p[:, P_PSCALE:P_PSCALE + 24] = fm(inp['pool_scale'])
    p[:, P_ONORM:P_ONORM + 2] = inp['delta_onorm'].T
    p[:, P_ALOG:P_ALOG + 24] = np.broadcast_to(inp['a_log'].reshape(1, 24), (128, 24))
    p[:, P_DTB:P_DTB + 24] = np.broadcast_to(inp['dt_bias'].reshape(1, 24), (128, 24))
    wins = np.array([2, 4, 8, 16], np.float32)
    t = np.arange(16, dtype=np.float32)
    cnt0 = np.minimum(wins[:, None], t[None, :] + 1.0)
    cnt1 = np.broadcast_to(wins[:, None], (4, 16))
    tab = np.stack([1.0 / cnt0, 1.0 / cnt1]).astype(np.float32)
    p[:, P_CNT:P_CNT + 128] = np.broadcast_to(tab.reshape(1, 128), (128, 128))
    return p


def make_in_map(inp, core, consts, params):
    b = core % 4
    m = {}
    xT = np.empty((NPASS, D, T), np.float32)
    for ps_ in range(NPASS):
        xT[ps_, :, :TP] = inp['x_prompt'][b, ps_ * TP:(ps_ + 1) * TP, :].T
        sq0 = b * 32 + ps_ * NSQ
        xT[ps_, :, TP:] = inp['x_sample'][sq0:sq0 + NSQ].reshape(TS, D).T
    m['xT'] = xT
    m['memT'] = np.ascontiguousarray(inp['mem_prompt'][b].T)
    sl = [slice(b * 32 + ps_ * NSQ, b * 32 + (ps_ + 1) * NSQ) for ps_ in range(NPASS)]
    m['sS'] = np.stack([inp['state_delta_S'][:, s] for s in sl])
    sc = np.stack([inp['state_delta_conv'][:, s] for s in sl])
    m['sconvT'] = np.ascontiguousarray(sc.reshape(NPASS, 2, NSQ, 3, 36, 128).transpose(0, 1, 4, 5, 2, 3)).reshape(NPASS, 2, 36, 128, NSQ * 3)
    sp = np.stack([inp['state_pool'][:, s] for s in sl])
    m['spoolT'] = np.ascontiguousarray(sp.reshape(NPASS, 2, NSQ, 15, 12, 128).transpose(0, 1, 4, 5, 2, 3)).reshape(NPASS, 2, 12, 128, NSQ * 15)
    ck = np.stack([inp['cache_mem_k'][:, s] for s in sl])
    m['ckT'] = np.ascontiguousarray(ck.transpose(0, 1, 2, 5, 4, 3)).reshape(NPASS, DEPTH, NSQ, 128, 1024)
    m['cv'] = np.stack([inp['cache_mem_v'][:, s] for s in sl]).reshape(NPASS, DEPTH, NSQ, 256, 512)
    m['params'] = params
    m['consts'] = consts
    for k in ('w_mem_kv', 'w_out', 'w_gate_up', 'w_down'):
        m[k] = inp[k][:RUN_DEPTH]
    m['w_in_delta'] = inp['w_in_delta'][:W_ND]
    m['w_in_pool'] = inp['w_in_pool'][:W_NP]
    m['w_pool_grp'] = inp['w_pool_grp'][:W_NP]
    return m


def kernel(**inp):
    inp = {k: np.asarray(v) for k, v in inp.items()}
    nc = build_nc()
    consts = make_consts()
    params = make_params(inp)
    in_maps = [make_in_map(inp, core, consts, params) for core in range(8)]
    res = run_bass_kernel_spmd(nc, in_maps, core_ids=list(range(8)))
    R = res.results
    y_prompt = np.empty((4, 2048, D), np.float32)
    y_sample = np.empty((128, 4, D), np.float32)
    p_S = np.empty((2, 4, 12, 128, 128), np.float32)
    p_conv = np.empty((2, 4, 3, 4608), np.float32)
    p_pool = np.empty((2, 4, 15, 1536), np.float32)
    p_mk = np.empty((4, 4, 256, 4, 128), np.float32)
    p_mv = np.empty((4, 4, 256, 4, 128), np.float32)
    s_S = np.empty((2, 128, 12, 128, 128), np.float32)
    s_conv = np.empty((2, 128, 3, 4608), np.float32)
    s_pool = np.empty((2, 128, 15, 1536), np.float32)
    for b in range(4):
        r = R[b]
        for ps_ in range(NPASS):
            yT = r['yT'][ps_]
            y_prompt[b, ps_ * TP:(ps_ + 1) * TP, :] = yT[:, :TP].T
            sq0 = b * 32 + ps_ * NSQ
            y_sample[sq0:sq0 + NSQ] = yT[:, TP:].T.reshape(NSQ, 4, D)
            s_S[:, sq0:sq0 + NSQ] = r['o_sS'][ps_]
            s_conv[:, sq0:sq0 + NSQ] = r['o_sconvT'][ps_].reshape(2, 36, 128, NSQ, 3).transpose(0, 3, 4, 1, 2).reshape(2, NSQ, 3, 4608)
            s_pool[:, sq0:sq0 + NSQ] = r['o_spoolT'][ps_].reshape(2, 12, 128, NSQ, 15).transpose(0, 3, 4, 1, 2).reshape(2, NSQ, 15, 1536)
        p_S[:, b] = r['o_pS']
        p_conv[:, b] = r['o_pconvT'].transpose(0, 3, 1, 2).reshape(2, 3, 4608)
        p_pool[:, b] = r['o_ppoolT'].transpose(0, 3, 1, 2).reshape(2, 15, 1536)
        p_mk[:, b] = r['o_mkT'].reshape(4, 128, 4, 256).transpose(0, 3, 2, 1)
        p_mv[:, b] = r['o_mv'].reshape(4, 256, 4, 128)
    return (y_prompt, y_sample, p_S, p_conv, p_pool, p_mk, p_mv, s_S, s_conv, s_pool)
```

```python
import numpy as np
from contextlib import ExitStack
import concourse.bass as bass
import concourse.mybir as mybir
from concourse.bass_utils import run_bass_kernel_spmd

F32 = mybir.dt.float32
BF16 = mybir.dt.bfloat16
AF = mybir.ActivationFunctionType
ALU = mybir.AluOpType

D = 2048
T = 1088
TP = 1024
NSQ = 16
TS = 64
TT = [(0, 384, 384), (384, 384, 384), (768, 320, 256)]
import os
DEPTH = 4
NPASS = 2
RUN_DEPTH = int(os.environ.get("K_DEPTH", "4"))
RUN_NPASS = int(os.environ.get("K_NPASS", "2"))
RUN_SKIP = os.environ.get("K_SKIP", "")
W_ND = (RUN_DEPTH + 1) // 2
W_NP = max(1, RUN_DEPTH // 2)
DFF = 5632
IN_DELTA = 6680
EPS = 1e-6
NBUF = 4
EPOCH = 12000
NQ = 4
FQ = 11

P_GMIX = 0
P_GFFN = 64
P_GMEM = 128
P_GFIN = 192
P_CONVW = 208
P_PSCALE = P_CONVW + 288
P_ONORM = P_PSCALE + 24
P_ALOG = P_ONORM + 2
P_DTB = P_ALOG + 24
P_CNT = P_DTB + 24
NP_ = P_CNT + 128
C_ID = 0
C_ONES = 128
C_AVG = 256
C_U = 384
C_MBSL = 512
C_MNUI = 640
C_US = 768
C_BMS = 896
C_MBSLS = 1024
C_MNUIS = 1152
C_SEL = 1280
C_EPS = C_SEL + 16
NC_ = C_EPS + 1


class Sched:
    ENGS = ('pe', 'act', 'dve', 'pool', 'sp')

    def __init__(self, nc):
        self.nc = nc
        self.streams = {e: [] for e in self.ENGS}
        self.cnt = {}
        self.epoch = {}
        self.lastw = {}
        self.readers = {}
        self.keys = []
        self.lastdma = {}

    def _newtok(self, base, inc):
        ep = self.epoch.get(base, 0)
        key = (base, ep)
        c = self.cnt.get(key, 0)
        if c + inc > EPOCH:
            ep += 1
            self.epoch[base] = ep
            key = (base, ep)
            c = 0
        if key not in self.cnt:
            self.keys.append(key)
        self.cnt[key] = c + inc
        return (key, c + inc)

    def inherit(self, new, olds):
        m = {}
        for o in olds:
            t = self.lastw.get(o)
            if t is not None and m.get(t[0], 0) < t[1]:
                m[t[0]] = t[1]
            for k, v in self.readers.get(o, {}).items():
                if m.get(k, 0) < v:
                    m[k] = v
        self.readers[new] = m
        self.lastw[new] = None

    def op(self, eng, fn, reads=(), writes=(), dma=None):
        deps = {}

        def add(t):
            if t is None:
                return
            k, v = t
            if deps.get(k, 0) < v:
                deps[k] = v
        for r in reads:
            add(self.lastw.get(r))
            if r.startswith('ps'):
                for k, v in self.readers.get(r, {}).items():
                    add((k, v))
        for w in writes:
            add(self.lastw.get(w))
            for k, v in self.readers.get(w, {}).items():
                add((k, v))
        if dma is None:
            tok = self._newtok(eng, 1)
        else:
            add(self.lastdma.get(dma))
            tok = self._newtok('dma_' + dma, 16)
            self.lastdma[dma] = tok
        self.streams[eng].append((deps, fn, tok, dma is not None))
        for r in reads:
            d = self.readers.setdefault(r, {})
            if d.get(tok[0], 0) < tok[1]:
                d[tok[0]] = tok[1]
        for w in writes:
            self.lastw[w] = tok
            self.readers[w] = {}
        return tok

    def emit(self):
        nc = self.nc
        final = [(k, self.cnt[k]) for k in self.keys]
        with ExitStack() as es:
            sems = {}
            for i, key in enumerate(self.keys):
                sems[key] = es.enter_context(nc.semaphore("s%d" % i))
            block = es.enter_context(nc.Block())

            def run(name, e):
                waited = {}
                for deps, fn, tok, isdma in self.streams[name]:
                    for k, v in deps.items():
                        if k[0] == 'pe' and name == 'pe':
                            continue
                        if waited.get(k, 0) >= v:
                            continue
                        e.wait_ge(sems[k], v)
                        waited[k] = v
                    ins = fn(e)
                    ins.then_inc(sems[tok[0]], 16 if isdma else 1)
                if name == 'sp':
                    for k, v in final:
                        e.wait_ge(sems[k], v)

            @block.tensor
            def _(e):
                run('pe', e)

            @block.scalar
            def _(e):
                run('act', e)

            @block.vector
            def _(e):
                run('dve', e)

            @block.gpsimd
            def _(e):
                run('pool', e)

            @block.sync
            def _(e):
                run('sp', e)


class Builder:
    def __init__(self, plan=None):
        self.plan = plan
        self.rec = []
        self.recording = plan is None

    def build(self):
        nc = bass.Bass("TRN2", target_bir_lowering=False)
        self.nc = nc
        S = Sched(nc)
        self.S = S
        dt = nc.dram_tensor

        def inp(name, shape):
            return dt(name, shape, F32, kind="ExternalInput").ap()

        def outp(name, shape):
            return dt(name, shape, F32, kind="ExternalOutput").ap()

        def scr(name, shape):
            return dt(name, shape, F32, kind="Internal").ap()
        self.d_x = inp("xT", [NPASS, D, T])
        self.d_mem = inp("memT", [D, 256])
        self.d_sS = inp("sS", [NPASS, 2, NSQ, 12, 128, 128])
        self.d_sconv = inp("sconvT", [NPASS, 2, 36, 128, NSQ * 3])
        self.d_spool = inp("spoolT", [NPASS, 2, 12, 128, NSQ * 15])
        self.d_ck = inp("ckT", [NPASS, DEPTH, NSQ, 128, 4 * 256])
        self.d_cv = inp("cv", [NPASS, DEPTH, NSQ, 256, 512])
        self.d_par = inp("params", [128, NP_])
        self.d_con = inp("consts", [128, NC_])
        self.w_in_delta = inp("w_in_delta", [W_ND, D, IN_DELTA])
        self.w_in_pool = inp("w_in_pool", [W_NP, D, D])
        self.w_pool_grp = inp("w_pool_grp", [W_NP, 4, 384, 384])
        self.w_mem_kv = inp("w_mem_kv", [RUN_DEPTH, D, 1024])
        self.w_out = inp("w_out", [RUN_DEPTH, D, D])
        self.w_gate_up = inp("w_gate_up", [RUN_DEPTH, D, 2 * DFF])
        self.w_down = inp("w_down", [RUN_DEPTH, DFF, D])

        self.o_y = outp("yT", [NPASS, D, T])
        self.o_pS = outp("o_pS", [2, 12, 128, 128])
        self.o_pconv = outp("o_pconvT", [2, 36, 128, 3])
        self.o_ppool = outp("o_ppoolT", [2, 12, 128, 15])
        self.o_mk = outp("o_mkT", [DEPTH, 128, 4 * 256])
        self.o_mv = outp("o_mv", [DEPTH, 256, 512])
        self.o_sS = outp("o_sS", [NPASS, 2, NSQ, 12, 128, 128])
        self.o_sconv = outp("o_sconvT", [NPASS, 2, 36, 128, NSQ * 3])
        self.o_spool = outp("o_spoolT", [NPASS, 2, 12, 128, NSQ * 15])
        self.dbg = os.environ.get("K_DBG", "0") == "1"
        if self.dbg:
            self.o_dbg = dt("dbg_y", [128, 16, T], BF16, kind="ExternalOutput").ap()
        self.c_S = scr("scr_S", [2, 12, 128, 128])
        self.c_conv = scr("scr_conv", [2, 36, 128, 3])
        self.c_pool = scr("scr_pool", [2, 12, 128, 15])

        with ExitStack() as es:
            sb = lambda name, shape, dtp: es.enter_context(nc.sbuf_tensor(name, shape, dtp))[:]
            self.xT = sb("xT_sb", [128, 16, T], F32)
            self.hT = sb("hT_sb", [128, 16, T], BF16)
            self.yb = sb("yb_sb", [128, 16, T], BF16)
            self.wsl = [sb("ws%d" % i, [128, 2048], BF16) for i in range(NBUF)]
            self.par = sb("par_sb", [128, NP_], F32)
            self.con = sb("con_sb", [128, NC_], F32)
            self.onesb = sb("onesb", [128, 128], BF16)
            self.kTb = sb("kTb", [128, 4, 256], BF16)
            self.vb = sb("vb", [128, 2, 512], BF16)
            self.AR = (nc.sbuf_bytes_remaining - int(os.environ.get("K_RES", "4096"))) // 4
            self.arena = sb("arena", [128, self.AR], F32)
            self.ps = [es.enter_context(nc.psum_tensor("ps%d" % i, [128, 512], F32))[:] for i in range(8)]
            self.psi = 0
            self.ws_i = 0
            self.ws_issued = 0
            self.ws_rel = 0
            self.ar_off = 0
            self.ar_names = []
            self.ar_prev = []
            self.dmarr = 0
            self.program()
            if not self.recording:
                S.emit()
        return nc

    def bank(self):
        i = self.psi % 8
        self.psi += 1
        return self.ps[i], 'ps%d' % i

    def phase(self):
        S = self.S
        m = dict(getattr(self, 'fence', {}))
        for o in self.ar_names:
            t = S.lastw.get(o)
            if t is not None and m.get(t[0], 0) < t[1]:
                m[t[0]] = t[1]
            for k, v in S.readers.get(o, {}).items():
                if m.get(k, 0) < v:
                    m[k] = v
        self.fence = m
        self.ar_names = []
        self.ar_off = 0

    def alloc(self, name, n):
        assert self.ar_off + n <= self.AR, ("arena overflow", name, self.ar_off + n, self.AR)
        ap = self.arena[:, self.ar_off:self.ar_off + n]
        self.ar_off += n
        self._uid = getattr(self, '_uid', 0) + 1
        rn = "%s#%d" % (name, self._uid)
        self.S.readers[rn] = dict(getattr(self, 'fence', {}))
        self.S.lastw[rn] = None
        self.ar_names.append(rn)
        return ap, rn

    def rsqrt(self, out_ap, out_r, pin, pin_r, scale=1.0):
        S = self.S
        S.op('act', lambda e: e.activation(out=out_ap, in_=pin, func=AF.Ln, bias=self.con[:, C_EPS:C_EPS + 1], scale=scale),
             reads=[pin_r, 'con'], writes=[out_r])
        S.op('act', lambda e: e.activation(out=out_ap, in_=out_ap, func=AF.Exp, scale=-0.5), reads=[out_r], writes=[out_r])

    def dq(self):
        self.dmarr += 1
        return 'q%d' % (self.dmarr % 16)

    def C(self, off, n=128, rows=128):
        return self.con[0:rows, off:off + n]

    def Pc(self, off, n=1):
        return self.par[:, off:off + n]

    def panel(self, src, nk, ncols):
        idx = self.ws_i
        self.ws_i += 1
        sl = idx % NBUF
        if self.recording:
            self.rec.append((src, nk, ncols))
        else:
            assert idx < self.ws_rel + NBUF, "too many held panels"
            self._ws_issue()
            assert self.ws_issued > idx
        return self.wsl[sl][:, 0:nk * ncols].rearrange("p (k m) -> p k m", m=ncols), 'ws%d' % sl

    def _ws_issue(self):
        S = self.S
        while self.ws_issued < min(len(self.plan), self.ws_rel + NBUF):
            j = self.ws_issued
            psrc, pnk, pnc = self.plan[j]
            sl = j % NBUF
            dst = self.wsl[sl][:, 0:pnk * pnc].rearrange("p (k m) -> p k m", m=pnc)
            S.op('pool', lambda e, dst=dst, psrc=psrc: e.dma_start(out=dst, in_=psrc),
                 writes=['ws%d' % sl], dma='ws%d' % sl)
            self.ws_issued += 1

    def release(self, n=1):
        self.ws_rel = getattr(self, 'ws_rel', 0) + n
        if not self.recording:
            self._ws_issue()

    def wview(self, w2d, c0, ncols, k0=0, nk=16):
        return w2d[k0 * 128:(k0 + nk) * 128, c0:c0 + ncols].rearrange("(k p) m -> p k m", p=128)

    def mm_group(self, out_ap, pairs, reads, wres):
        n = len(pairs)

        def fn(e):
            ins = None
            for i, (l, r) in enumerate(pairs):
                ins = e.matmul(out_ap, lhsT=l, rhs=r, start=(i == 0), stop=(i == n - 1))
            return ins
        return self.S.op('pe', fn, reads=reads, writes=[wres])

    def rmsnorm_to_h(self, gcol):
        S = self.S
        self.phase()
        sq, sq_r = self.alloc("nsq", T)
        rs, rs_r = self.alloc("nrs", T)
        xr = lambda c, ti: 'x%d_%d' % (c, ti)
        banks = [self.bank() for _ in TT]
        for c in range(16):
            S.op('act', lambda e, c=c: e.activation(out=sq, in_=self.xT[:, c, :], func=AF.Square),
                 reads=[xr(c, ti) for ti in range(3)], writes=[sq_r])
            for ti, (t0, tn, _) in enumerate(TT):
                pb, pr = banks[ti]
                S.op('pe', lambda e, pb=pb, t0=t0, tn=tn, c=c: e.matmul(
                    pb[:, 0:tn], lhsT=self.C(C_AVG), rhs=sq[:, t0:t0 + tn], start=(c == 0), stop=(c == 15)),
                    reads=[sq_r, 'con'], writes=[pr])
        for ti, (t0, tn, _) in enumerate(TT):
            pb, pr = banks[ti]
            self.rsqrt(rs[:, t0:t0 + tn], rs_r, pb[:, 0:tn], pr)
        for c in range(16):
            S.op('dve', lambda e, c=c: e.scalar_tensor_tensor(
                out=self.hT[:, c, :], in0=self.xT[:, c, :], scalar=self.Pc(gcol + c), in1=rs,
                op0=ALU.mult, op1=ALU.mult),
                reads=[xr(c, ti) for ti in range(3)] + [rs_r, 'par'], writes=['h%d' % c])

    def gemm_x_update(self, wsrc_fn, nk, rhs_fn, rhs_reads):
        S = self.S
        for m in range(16):
            wp, wr = self.panel(wsrc_fn(m), nk, 128)
            for ti, (t0, tn, _) in enumerate(TT):
                pb, pr = self.bank()
                self.mm_group(pb[:, 0:tn], [(wp[:, k, :], rhs_fn(k, t0, tn)) for k in range(nk)],
                              reads=[wr] + rhs_reads, wres=pr)
                xres = 'x%d_%d' % (m, ti)
                S.op('dve', lambda e, pb=pb, m=m, t0=t0, tn=tn: e.tensor_tensor(
                    out=self.xT[:, m, t0:t0 + tn], in0=pb[:, 0:tn], in1=self.xT[:, m, t0:t0 + tn], op=ALU.add),
                    reads=[pr, xres], writes=[xres])
            self.release(1)

    def ffn(self, l):
        S = self.S
        hreads = ['h%d' % c for c in range(16)]
        for q in range(NQ):
            for f in range(FQ):
                fc = q * FQ + f
                wg, wgr = self.panel(self.wview(self.w_gate_up[l], fc * 128, 128), 16, 128)
                wu, wur = self.panel(self.wview(self.w_gate_up[l], DFF + fc * 128, 128), 16, 128)
                for ti, (t0, tn, _) in enumerate(TT):
                    pg, pgr = self.bank()
                    pu, pur = self.bank()
                    self.mm_group(pg[:, 0:tn], [(wg[:, k, :], self.hT[:, k, t0:t0 + tn]) for k in range(16)],
                                  reads=[wgr] + hreads, wres=pgr)
                    self.mm_group(pu[:, 0:tn], [(wu[:, k, :], self.hT[:, k, t0:t0 + tn]) for k in range(16)],
                                  reads=[wur] + hreads, wres=pur)
                    sg, sgr = self.sgbuf[(fc * 3 + ti) % 2]
                    S.op('act', lambda e, pg=pg, sg=sg, tn=tn: e.activation(out=sg[:, 0:tn], in_=pg[:, 0:tn], func=AF.Silu),
                         reads=[pgr], writes=[sgr])
                    S.op('dve', lambda e, pu=pu, sg=sg, f=f, t0=t0, tn=tn: e.tensor_tensor(
                        out=self.yb[:, f, t0:t0 + tn], in0=pu[:, 0:tn], in1=sg[:, 0:tn], op=ALU.mult),
                        reads=[pur, sgr], writes=['y%d' % f])
                self.release(2)
            self.gemm_x_update(lambda m, q=q: self.wview(self.w_down[l], m * 128, 128, k0=q * FQ, nk=FQ), FQ,
                               lambda k, t0, tn: self.yb[:, k, t0:t0 + tn], ['y%d' % f for f in range(FQ)])

    def mem_kv(self, l, write_out):
        S = self.S
        self.phase()
        mh, mh_r = self.alloc("memhat", 16 * 128)
        mhb = mh.bitcast(BF16).rearrange("p (c m) -> p c m", m=256)
        mf, mf_r = self.alloc("memf", 16 * 128)
        mfv = mf.rearrange("p (c m) -> p c m", m=128)
        sq, sq_r = self.alloc("memsq", 128)
        rs, rs_r = self.alloc("memrs", 128)
        ko, ko_r = self.alloc("memko", 1024)
        vo, vo_r = self.alloc("memvo", 1024)
        for mt0 in range(2):
            S.op('sp', lambda e, mt0=mt0: e.dma_start(
                out=mfv, in_=self.d_mem[:, mt0 * 128:(mt0 + 1) * 128].rearrange("(c p) m -> p c m", p=128)),
                writes=[mf_r], dma=self.dq())
            pb, pr = self.bank()
            for c in range(16):
                S.op('act', lambda e, c=c: e.activation(out=sq, in_=mfv[:, c, :], func=AF.Square), reads=[mf_r], writes=[sq_r])
                S.op('pe', lambda e, c=c, pb=pb: e.matmul(pb[:, 0:128], lhsT=self.C(C_AVG), rhs=sq, start=(c == 0), stop=(c == 15)),
                     reads=[sq_r, 'con'], writes=[pr])
            self.rsqrt(rs, rs_r, pb[:, 0:128], pr)
            for c in range(16):
                S.op('dve', lambda e, c=c, mt0=mt0: e.scalar_tensor_tensor(
                    out=mhb[:, c, mt0 * 128:(mt0 + 1) * 128], in0=mfv[:, c, :],
                    scalar=self.Pc(P_GMEM + l * 16 + c), in1=rs, op0=ALU.mult, op1=ALU.mult),
                    reads=[mf_r, rs_r, 'par'], writes=[mh_r])
        kov = ko.rearrange("p (h m) -> p h m", m=256)
        if 'mkvout' in RUN_SKIP:
            write_out = False
        for hm in range(4 if 'mkvK' not in RUN_SKIP else 0):
            wp, wr = self.panel(self.wview(self.w_mem_kv[l], hm * 128, 128), 16, 128)
            pb, pr = self.bank()
            self.mm_group(pb[:, 0:256], [(wp[:, k, :], mhb[:, k, :]) for k in range(16)], reads=[wr, mh_r], wres=pr)
            S.op('act', lambda e, pb=pb, hm=hm: e.copy(out=self.kTb[:, hm, :], in_=pb[:, 0:256]), reads=[pr], writes=['kTb'])
            if write_out:
                S.op('dve', lambda e, pb=pb, hm=hm: e.tensor_copy(out=kov[:, hm, :], in_=pb[:, 0:256]), reads=[pr], writes=[ko_r])
            self.release(1)
        if 'mkvV' in RUN_SKIP:
            return
        wv = [self.panel(self.wview(self.w_mem_kv[l], 512 + j * 128, 128), 16, 128) for j in range(4)]
        vov = vo.rearrange("p (t c) -> p t c", c=512)
        for mt in range(2):
            pb, pr = self.bank()
            for j in range(4):
                wp, wr = wv[j]
                self.mm_group(pb[:, j * 128:(j + 1) * 128],
                              [(mhb[:, k, mt * 128:(mt + 1) * 128], wp[:, k, :]) for k in range(16)],
                              reads=[wr, mh_r], wres=pr)
            S.op('act', lambda e, pb=pb, mt=mt: e.copy(out=self.vb[:, mt, :], in_=pb[:, 0:512]), reads=[pr], writes=['vb'])
            if write_out:
                S.op('dve', lambda e, pb=pb, mt=mt: e.tensor_copy(out=vov[:, mt, :], in_=pb[:, 0:512]), reads=[pr], writes=[vo_r])
        self.release(4)
        if write_out and 'mko' not in RUN_SKIP:
            S.op('sp', lambda e: e.dma_start(out=self.o_mk[l], in_=ko), reads=[ko_r], dma=self.dq())
        if write_out and 'mvo' not in RUN_SKIP:
            S.op('sp', lambda e: e.dma_start(out=self.o_mv[l].rearrange("(t p) c -> p t c", p=128), in_=vov),
                 reads=[vo_r], dma=self.dq())

    def xattn_prompt_tile(self, qm, qm_r, hm, ti):
        S = self.S
        t0, tn, tnp = TT[ti]
        e_aps = []
        for mt in range(2):
            pb, pr = self.bank()
            self.mm_group(pb[:, 0:tnp], [(self.kTb[:, hm, mt * 128:(mt + 1) * 128], qm[:, 0:tnp])], reads=['kTb', qm_r], wres=pr)
            eb, eb_r = self.ebuf[mt]
            S.op('act', lambda e, pb=pb, eb=eb, tnp=tnp: e.activation(out=eb[:, 0:tnp], in_=pb[:, 0:tnp], func=AF.Exp,
                                                                     scale=128.0 ** -0.5), reads=[pr], writes=[eb_r])
            e_aps.append((eb, eb_r))
        po, por = self.bank()
        pd, pdr = self.bank()
        self.mm_group(po[:, 0:tnp], [(self.vb[:, mt, hm * 128:(hm + 1) * 128], e_aps[mt][0][:, 0:tnp]) for mt in range(2)],
                      reads=['vb', e_aps[0][1], e_aps[1][1]], wres=por)
        self.mm_group(pd[:, 0:tnp], [(self.onesb[:, :], e_aps[mt][0][:, 0:tnp]) for mt in range(2)],
                      reads=['onesb', e_aps[0][1], e_aps[1][1]], wres=pdr)
        rd, rd_r = self.rdbuf
        S.op('dve', lambda e, pd=pd, tnp=tnp: e.reciprocal(out=rd[:, 0:tnp], in_=pd[:, 0:tnp]), reads=[pdr], writes=[rd_r])
        S.op('dve', lambda e, po=po, t0=t0, tnp=tnp, hm=hm: e.tensor_tensor(
            out=self.yb[:, 12 + hm, t0:t0 + tnp], in0=po[:, 0:tnp], in1=rd[:, 0:tnp], op=ALU.mult),
            reads=[por, rd_r], writes=['y%d' % (12 + hm)])

    def qmem_and_prompt_attn(self, wsrc_fn):
        S = self.S
        qs, qs_r = self.alloc("qs", 4 * TS)
        qsv = qs.rearrange("p (h t) -> p h t", t=TS)
        qm, qm_r = self.alloc("qm", 192)
        qmb = qm.bitcast(BF16)
        e0, e0r = self.alloc("e0", 192)
        e1, e1r = self.alloc("e1", 192)
        self.ebuf = [(e0.bitcast(BF16), e0r), (e1.bitcast(BF16), e1r)]
        self.rdbuf = self.alloc("rd", 384)
        hreads = ['h%d' % c for c in range(16)]
        for hm in range(4):
            wp, wr = self.panel(wsrc_fn(hm), 16, 128)
            for ti, (t0, tn, tnp) in enumerate(TT):
                pb, pr = self.bank()
                self.mm_group(pb[:, 0:tn], [(wp[:, k, :], self.hT[:, k, t0:t0 + tn]) for k in range(16)],
                              reads=[wr] + hreads, wres=pr)
                S.op('act', lambda e, pb=pb, tnp=tnp: e.copy(out=qmb[:, 0:tnp], in_=pb[:, 0:tnp]), reads=[pr], writes=[qm_r])
                if ti == 2:
                    S.op('dve', lambda e, pb=pb, hm=hm: e.tensor_copy(out=qsv[:, hm, :], in_=pb[:, 256:320]),
                         reads=[pr], writes=[qs_r])
                self.xattn_prompt_tile(qmb, qm_r, hm, ti)
            self.release(1)
        return qsv, qs_r

    def xattn_sample(self, l, ps_, qsv, qs_r):
        S = self.S
        kt = [self.alloc("skt%d" % i, 1024) for i in range(2)]
        vt = [self.alloc("svt%d" % i, 1024) for i in range(2)]
        ee = [self.alloc("see%d" % i, 32) for i in range(2)]
        rd = [self.alloc("srd%d" % i, 16) for i in range(2)]
        for s in range(NSQ):
            k_ap, k_r = kt[s % 2]
            v_ap, v_r = vt[s % 2]
            e_ap, e_r = ee[s % 2]
            r_ap, r_r = rd[s % 2]
            kv = k_ap.rearrange("p (h m) -> p h m", m=256)
            vv = v_ap.rearrange("p (t c) -> p t c", c=512)
            ev = e_ap.rearrange("p (h t q) -> p h t q", t=2, q=4)
            S.op('sp', lambda e, k_ap=k_ap, s=s: e.dma_start(out=k_ap, in_=self.d_ck[ps_, l, s]), writes=[k_r], dma=self.dq())
            S.op('sp', lambda e, vv=vv, s=s: e.dma_start(out=vv, in_=self.d_cv[ps_, l, s].rearrange("(t p) c -> p t c", p=128)),
                 writes=[v_r], dma=self.dq())
            pb, pr = self.bank()

            def sc(e, pb=pb, kv=kv, s=s):
                ins = None
                for hm in range(4):
                    for mt in range(2):
                        o = (hm * 2 + mt) * 4
                        ins = e.matmul(pb[:, o:o + 4], lhsT=kv[:, hm, mt * 128:(mt + 1) * 128],
                                       rhs=qsv[:, hm, s * 4:s * 4 + 4], start=True, stop=True)
                return ins
            S.op('pe', sc, reads=[k_r, qs_r], writes=[pr])
            S.op('act', lambda e, pb=pb, e_ap=e_ap: e.activation(out=e_ap, in_=pb[:, 0:32], func=AF.Exp, scale=128.0 ** -0.5),
                 reads=[pr], writes=[e_r])
            po, por = self.bank()

            def pv(e, po=po, vv=vv, ev=ev):
                ins = None
                for hm in range(4):
                    for mt in range(2):
                        ins = e.matmul(po[:, hm * 4:hm * 4 + 4], lhsT=vv[:, mt, hm * 128:(hm + 1) * 128],
                                       rhs=ev[:, hm, mt, :], start=(mt == 0), stop=(mt == 1))
                for mt in range(2):
                    ins = e.matmul(po[:, 16:32].rearrange("p (h q) -> p h q", q=4), lhsT=self.C(C_ONES),
                                   rhs=ev[:, :, mt, :], start=(mt == 0), stop=(mt == 1))
                return ins
            S.op('pe', pv, reads=[v_r, e_r, 'con'], writes=[por])
            S.op('dve', lambda e, po=po, r_ap=r_ap: e.reciprocal(out=r_ap, in_=po[:, 16:32]), reads=[por], writes=[r_r])
            S.op('dve', lambda e, po=po, r_ap=r_ap, s=s: e.tensor_tensor(
                out=self.yb[:, 12:16, TP + s * 4:TP + s * 4 + 4], in0=po[:, 0:16].rearrange("p (h q) -> p h q", q=4),
                in1=r_ap.rearrange("p (h q) -> p h q", q=4), op=ALU.mult),
                reads=[por, r_r], writes=['y12', 'y13', 'y14', 'y15'])

    def pool_layer(self, l, ps_):
        S = self.S
        pi = l // 2
        self.phase()
        hreads = ['h%d' % c for c in range(16)]
        WP = 15 + TP + NSQ * 19
        up = [self.alloc("up%d" % i, WP) for i in range(2)]
        w1, w1_r = self.alloc("pw1", WP)
        w2, w2_r = self.alloc("pw2", WP)
        dbf = [self.alloc("pd%d" % i, T // 2) for i in range(3)]
        t16, t16_r = self.alloc("pt16", 16)
        wins = [2, 4, 8, 16]
        for g in range(4):
            for j in range(3):
                c = g * 3 + j
                u_ap, u_r = up[(g * 3 + j) % 2]
                usv = u_ap[:, 15 + TP:WP].rearrange("p (s r) -> p s r", r=19)
                if ps_ == 0:
                    S.op('dve', lambda e, u_ap=u_ap: e.memset(u_ap[:, 0:15], 0.0), writes=[u_r])
                else:
                    S.op('sp', lambda e, u_ap=u_ap, c=c: e.dma_start(out=u_ap[:, 0:15], in_=self.c_pool[pi, c]),
                         reads=['c_pool%d_%d' % (pi, c)], writes=[u_r], dma=self.dq())
                S.op('sp', lambda e, usv=usv, c=c: e.dma_start(
                    out=usv[:, :, 0:15], in_=self.d_spool[ps_, pi, c].rearrange("p (s r) -> p s r", r=15)),
                    writes=[u_r], dma=self.dq())
                wp, wr = self.panel(self.wview(self.w_in_pool[pi], c * 128, 128), 16, 128)
                for ti, (t0, tn, tnp) in enumerate(TT):
                    pb, pr = self.bank()
                    self.mm_group(pb[:, 0:tn], [(wp[:, k, :], self.hT[:, k, t0:t0 + tn]) for k in range(16)],
                                  reads=[wr] + hreads, wres=pr)
                    S.op('act', lambda e, pb=pb, u_ap=u_ap, t0=t0, tnp=tnp: e.copy(out=u_ap[:, 15 + t0:15 + t0 + tnp], in_=pb[:, 0:tnp]),
                         reads=[pr], writes=[u_r])
                    if ti == 2:
                        S.op('act', lambda e, pb=pb, usv=usv: e.copy(out=usv[:, :, 15:19],
                                                                      in_=pb[:, 256:320].rearrange("p (s q) -> p s q", q=4)),
                             reads=[pr], writes=[u_r])
                self.release(1)
                S.op('sp', lambda e, usv=usv, c=c: e.dma_start(
                    out=self.o_spool[ps_, pi, c].rearrange("p (s r) -> p s r", r=15), in_=usv[:, :, 4:19]),
                    reads=[u_r], dma=self.dq())
                if ps_ == 0:
                    S.op('sp', lambda e, u_ap=u_ap, c=c: e.dma_start(out=self.c_pool[pi, c], in_=u_ap[:, TP:TP + 15]),
                         reads=[u_r], writes=['c_pool%d_%d' % (pi, c)], dma=self.dq())
                else:
                    S.op('sp', lambda e, u_ap=u_ap, c=c: e.dma_start(out=self.o_ppool[pi, c], in_=u_ap[:, TP:TP + 15]),
                         reads=[u_r], dma=self.dq())
                src, src_r = u_ap, u_r
                sh = 1
                k = 0
                while sh < wins[g]:
                    dst, dst_r = (w1, w1_r) if k % 2 == 0 else (w2, w2_r)
                    eng = 'dve'
                    S.op(eng, lambda e, dst=dst, src=src, sh=sh: e.tensor_tensor(
                        out=dst[:, sh:WP], in0=src[:, sh:WP], in1=src[:, 0:WP - sh], op=ALU.add),
                        reads=[src_r], writes=[dst_r])
                    src, src_r = dst, dst_r
                    sh *= 2
                    k += 1
                d_ap, d_r = dbf[j]
                db = d_ap.bitcast(BF16)
                iw = 1.0 / wins[g]
                S.op('dve', lambda e, db=db, src=src, u_ap=u_ap, iw=iw: e.scalar_tensor_tensor(
                    out=db[:, 0:TP], in0=src[:, 15:15 + TP], scalar=iw, in1=u_ap[:, 15:15 + TP],
                    op0=ALU.mult, op1=ALU.subtract), reads=[src_r, u_r], writes=[d_r])
                ssv = src[:, 15 + TP:WP].rearrange("p (s r) -> p s r", r=19)
                S.op('dve', lambda e, db=db, ssv=ssv, usv=usv, iw=iw: e.scalar_tensor_tensor(
                    out=db[:, TP:T].rearrange("p (s q) -> p s q", q=4), in0=ssv[:, :, 15:19], scalar=iw, in1=usv[:, :, 15:19],
                    op0=ALU.mult, op1=ALU.subtract), reads=[src_r, u_r], writes=[d_r])
                cnt = self.Pc(P_CNT + ps_ * 64 + g * 16, 16)
                S.op('dve', lambda e, src=src, cnt=cnt: e.tensor_tensor(out=t16, in0=src[:, 15:31], in1=cnt, op=ALU.mult),
                     reads=[src_r, 'par'], writes=[t16_r])
                S.op('dve', lambda e, db=db, u_ap=u_ap: e.tensor_tensor(out=db[:, 0:16], in0=t16, in1=u_ap[:, 15:31], op=ALU.subtract),
                     reads=[t16_r, u_r], writes=[d_r])
            for mo in range(3):
                wp, wr = self.panel(self.w_pool_grp[pi, g][:, mo * 128:(mo + 1) * 128].rearrange("(k p) m -> p k m", p=128), 3, 128)
                co = g * 3 + mo
                for ti, (t0, tn, tnp) in enumerate(TT):
                    pb, pr = self.bank()
                    self.mm_group(pb[:, 0:tn], [(wp[:, k, :], dbf[k][0].bitcast(BF16)[:, t0:t0 + tn]) for k in range(3)],
                                  reads=[wr] + [dbf[k][1] for k in range(3)], wres=pr)
                    S.op('act', lambda e, pb=pb, co=co, t0=t0, tn=tn: e.activation(
                        out=self.yb[:, co, t0:t0 + tn], in_=pb[:, 0:tn], func=AF.Identity, scale=self.Pc(P_PSCALE + pi * 12 + co)),
                        reads=[pr, 'par'], writes=['y%d' % co])
                self.release(1)
        self.phase()
        qsv, qs_r = self.qmem_and_prompt_attn(lambda hm: self.wview(self.w_in_pool[pi], 1536 + hm * 128, 128))
        self.xattn_sample(l, ps_, qsv, qs_r)

    def delta_layer(self, l, ps_):
        S = self.S
        di = l // 2
        self.phase()
        hreads = ['h%d' % c for c in range(16)]
        W = self.w_in_delta[di]
        A = lambda name, n: self.alloc(name, n)
        NCH = 9
        tg = {}
        for nm in ('g', 'beta', 'gc', 'ngam', 'ks', 'gend', 'tmp'):
            ap, r = A("tg_" + nm, NCH * 12)
            tg[nm] = (ap.rearrange("p (c h) -> p c h", h=12), r)
        wab, wabr = self.panel(self.wview(W, 6144, 24), 16, 24)
        pab, pabr = self.bank()
        pabv = pab[:, 0:NCH * 24].rearrange("p (c j) -> p c j", j=24)
        for ci in range(NCH):
            tok0 = ci * 128
            ntok = 128 if ci < 8 else 64
            self.mm_group(pabv[0:ntok, ci, :], [(self.hT[:, k, tok0:tok0 + ntok], wab[:, k, :]) for k in range(16)],
                          reads=[wabr] + hreads, wres=pabr)
        self.release(1)
        gv, g_r = tg['g']
        bv, b_r = tg['beta']
        tv, t_r = tg['tmp']
        alog = self.par[:, P_ALOG + di * 12:P_ALOG + di * 12 + 12]
        dtb = self.par[:, P_DTB + di * 12:P_DTB + di * 12 + 12]
        nA, nA_r = A("negA", 12)
        S.op('act', lambda e: e.activation(out=nA, in_=alog, func=AF.Exp), reads=['par'], writes=[nA_r])
        S.op('dve', lambda e: e.memset(gv, 0.0), writes=[g_r])
        S.op('dve', lambda e: e.memset(bv, 0.0), writes=[b_r])
        S.op('dve', lambda e: e.memset(tv, 0.0), writes=[t_r])
        for ci in range(NCH):
            nt = 128 if ci < 8 else 64
            S.op('dve', lambda e, ci=ci, nt=nt: e.tensor_tensor(out=tv[0:nt, ci, :], in0=pabv[0:nt, ci, 0:12], in1=dtb[0:nt, :], op=ALU.add),
                 reads=[pabr, 'par'], writes=[t_r])
            S.op('act', lambda e, ci=ci, nt=nt: e.activation(out=bv[0:nt, ci, :], in_=pabv[0:nt, ci, 12:24], func=AF.Sigmoid),
                 reads=[pabr], writes=[b_r])
        S.op('act', lambda e: e.activation(out=tv, in_=tv, func=AF.Exp), reads=[t_r], writes=[t_r])
        S.op('act', lambda e: e.activation(out=tv, in_=tv, func=AF.Ln, bias=1.0), reads=[t_r], writes=[t_r])
        for ci in range(NCH):
            S.op('dve', lambda e, ci=ci: e.scalar_tensor_tensor(out=gv[:, ci, :], in0=tv[:, ci, :], scalar=-1.0, in1=nA,
                                                                op0=ALU.mult, op1=ALU.mult), reads=[t_r, nA_r], writes=[g_r])
        S.op('dve', lambda e: e.memset(gv[64:128, 8, :], 0.0), writes=[g_r])
        gcv, gc_r = tg['gc']
        ngv, ng_r = tg['ngam']
        ksv, ks_r = tg['ks']
        gev, ge_r = tg['gend']
        pg1, pg1r = self.bank()
        pg2, pg2r = self.bank()
        p1v = pg1[:, 0:NCH * 12].rearrange("p (c h) -> p c h", h=12)
        p2v = pg2[:, 0:NCH * 12].rearrange("p (c h) -> p c h", h=12)
        for ci in range(NCH):
            Uc = self.C(C_U) if ci < 8 else self.C(C_US)
            Oc = self.C(C_ONES) if ci < 8 else self.C(C_BMS)
            S.op('pe', lambda e, ci=ci, Uc=Uc: e.matmul(p1v[:, ci, :], lhsT=Uc, rhs=gv[:, ci, :], start=True, stop=True),
                 reads=[g_r, 'con'], writes=[pg1r])
            S.op('pe', lambda e, ci=ci, Oc=Oc: e.matmul(p2v[:, ci, :], lhsT=Oc, rhs=gv[:, ci, :], start=True, stop=True),
                 reads=[g_r, 'con'], writes=[pg2r])
        S.op('dve', lambda e: e.tensor_copy(out=gcv, in_=p1v), reads=[pg1r], writes=[gc_r])
        S.op('act', lambda e: e.activation(out=ngv, in_=p1v, func=AF.Exp), reads=[pg1r], writes=[ng_r])
        S.op('dve', lambda e: e.tensor_scalar(out=ngv, in0=ngv, scalar1=-1.0, scalar2=None, op0=ALU.mult), reads=[ng_r], writes=[ng_r])
        S.op('act', lambda e: e.activation(out=gev, in_=p2v, func=AF.Exp), reads=[pg2r], writes=[ge_r])
        S.op('dve', lambda e: e.tensor_tensor(out=ksv, in0=p2v, in1=gcv, op=ALU.subtract), reads=[pg2r, gc_r], writes=[ks_r])
        S.op('act', lambda e: e.activation(out=ksv, in_=ksv, func=AF.Exp), reads=[ks_r], writes=[ks_r])

        pads = [A("pad%d" % i, 387) for i in range(3)]
        spads = [A("spad%d" % i, NSQ * 7) for i in range(3)]
        cb = [A("cb%d" % i, 384) for i in range(3)]
        zs, zs_r = A("zs", 384)
        ob, ob_r = A("ob", 384)
        t1, t1_r = A("t1", 384)
        t1s = [(t1, t1_r), A("t1k", 384)]
        CT = []
        for i in range(3):
            d_ = {}
            for nm in ('A0', 'A1', 'B0', 'B1', 'X0', 'X1', 'aT', 'XB', 'qg'):
                d_[nm] = A("c%d_%s" % (i, nm), 128)
            CT.append(d_)
        MT = {}
        for nm in ('Gam', 'kend', 'vtok', 'R', 'u'):
            MT[nm] = A("m_" + nm, 128)
        Sb = [A("S%d" % i, 128) for i in range(2)]
        GQ = 1
        sinB = [A("sin%d" % i, GQ * 128) for i in range(2)]
        soutB = [A("sout%d" % i, GQ * 128) for i in range(2)]
        umB = [A("um%d" % i, GQ * 128) for i in range(2)]
        gsel = A("gsel", 16)
        gends = A("gends", 16)
        qsv = None
        ID = self.C(C_ID)
        evk = [0]

        def evac(out, pin, reads, wres):
            evk[0] += 1
            if evk[0] % 2:
                return S.op('act', lambda e: e.copy(out=out, in_=pin), reads=reads, writes=[wres])
            return S.op('dve', lambda e: e.tensor_copy(out=out, in_=pin), reads=reads, writes=[wres])

        def bankfn(i):
            st = [0]

            def f():
                b = 2 * i + (st[0] % 2)
                st[0] += 1
                return self.ps[b], 'ps%d' % b
            return f

        def prep_gen(hd, ci, n, c0, sample, T_, bk):
            qn, qn_r = cb[0]
            kn, kn_r = cb[1]
            q_ = qn[:, c0:c0 + n]
            k_ = kn[:, c0:c0 + n]
            Um = self.C(C_U, n, n) if not sample else self.C(C_US, n, n)
            MB = self.C(C_MBSL, n, n) if not sample else self.C(C_MBSLS, n, n)
            MN = self.C(C_MNUI, n, n) if not sample else self.C(C_MNUIS, n, n)
            gcol = gcv[0:n, ci, hd:hd + 1]
            bcol = bv[0:n, ci, hd:hd + 1]
            tl = lambda nm: (T_[nm][0][0:n, 0:n], T_[nm][1])
            GU, GU_r = tl('A1')
            S.op('dve', lambda e: e.tensor_scalar(out=GU, in0=Um, scalar1=gv[0:n, ci, hd:hd + 1], scalar2=None, op0=ALU.mult),
                 reads=['con', g_r], writes=[GU_r])
            yield
            pG2, pG2r = bk()
            S.op('pe', lambda e: e.matmul(pG2[:, 0:n], lhsT=self.C(C_ONES, 128, n), rhs=GU, start=True, stop=True),
                 reads=[GU_r, 'con'], writes=[pG2r])
            yield
            E1, E1_r = tl('B1')
            E2, E2_r = tl('X1')
            S.op('dve', lambda e: e.scalar_tensor_tensor(out=E1, in0=pG2[0:n, 0:n], scalar=gcol, in1=MN, op0=ALU.subtract, op1=ALU.min),
                 reads=[pG2r, gc_r, 'con'], writes=[E1_r])
            yield
            S.op('dve', lambda e: e.scalar_tensor_tensor(out=E2, in0=pG2[0:n, 0:n], scalar=gcol, in1=MB, op0=ALU.subtract, op1=ALU.max),
                 reads=[pG2r, gc_r, 'con'], writes=[E2_r])
            yield
            qg, qg_r = (T_['qg'][0][:, 0:n], T_['qg'][1])
            if sample:
                Gam, Gam_r = (MT['Gam'][0][:, 0:n], MT['Gam'][1])
            else:
                Gam, Gam_r = qg, qg_r
            S.op('act', lambda e: e.activation(out=Gam, in_=pG2[:, 0:n], func=AF.Exp), reads=[pG2r], writes=[Gam_r])
            yield
            S.op('dve', lambda e: e.tensor_tensor(out=qg, in0=q_, in1=Gam, op=ALU.mult), reads=[qn_r, Gam_r], writes=[qg_r])
            yield
            S.op('act', lambda e: e.activation(out=E1, in_=E1, func=AF.Exp), reads=[E1_r], writes=[E1_r])
            yield
            S.op('act', lambda e: e.activation(out=E2, in_=E2, func=AF.Exp, scale=-1.0), reads=[E2_r], writes=[E2_r])
            yield
            pK, pKr = bk()
            S.op('pe', lambda e: e.matmul(pK[0:n, 0:n], lhsT=k_, rhs=k_, start=True, stop=True), reads=[kn_r], writes=[pKr])
            yield
            A0, A0_r = tl('A0')
            S.op('dve', lambda e: e.scalar_tensor_tensor(out=A0, in0=pK[0:n, 0:n], scalar=bcol, in1=E2,
                                                         op0=ALU.mult, op1=ALU.mult), reads=[pKr, b_r, E2_r], writes=[A0_r])
            yield
            pQ, pQr = bk()
            S.op('pe', lambda e: e.matmul(pQ[0:n, 0:n], lhsT=k_, rhs=q_, start=True, stop=True), reads=[kn_r, qn_r], writes=[pQr])
            yield
            aT, aT_r = tl('aT')
            S.op('dve', lambda e: e.tensor_tensor(out=aT, in0=pQ[0:n, 0:n], in1=E1, op=ALU.mult), reads=[pQr, E1_r], writes=[aT_r])
            yield
            pT, pTr = bk()
            S.op('pe', lambda e: e.matmul(pT[0:n, 0:n], lhsT=A0, rhs=ID[0:n, 0:n], start=True, stop=True), reads=[A0_r, 'con'], writes=[pTr])
            yield
            B0, B0_r = tl('B0')
            evac(B0, pT[0:n, 0:n], [pTr], B0_r)
            yield
            X0, X0_r = tl('X0')
            S.op('dve', lambda e: e.scalar_tensor_tensor(out=X0, in0=B0, scalar=-1.0, in1=ID[0:n, 0:n], op0=ALU.mult, op1=ALU.add),
                 reads=[B0_r, 'con'], writes=[X0_r])
            yield
            nlev = 1 if sample else 6
            Ac, Ac_r = A0, A0_r
            Bc, Bc_r = B0, B0_r
            Xc, Xc_r = X0, X0_r
            for lev in range(nlev):
                An, An_r = tl('A1') if lev % 2 == 0 else tl('A0')
                Bn, Bn_r = tl('B1') if lev % 2 == 0 else tl('B0')
                Xn, Xn_r = tl('X1') if lev % 2 == 0 else tl('X0')
                last = (lev == nlev - 1)
                pa, par_ = bk()
                S.op('pe', lambda e, pa=pa, Bc=Bc, Ac=Ac: e.matmul(pa[0:n, 0:n], lhsT=Bc, rhs=Ac, start=True, stop=True),
                     reads=[Ac_r, Bc_r], writes=[par_])
                yield
                if not last:
                    pb_, pbr_ = bk()
                    S.op('pe', lambda e, pb_=pb_, Bc=Bc, Ac=Ac: e.matmul(pb_[0:n, 0:n], lhsT=Ac, rhs=Bc, start=True, stop=True),
                         reads=[Ac_r, Bc_r], writes=[pbr_])
                    yield
                evac(An, pa[0:n, 0:n], [par_], An_r)
                yield
                if not last:
                    evac(Bn, pb_[0:n, 0:n], [pbr_], Bn_r)
                    yield
                px, pxr = bk()
                S.op('pe', lambda e, px=px, An=An, Xc=Xc: e.matmul(px[0:n, 0:n], lhsT=An, rhs=Xc, start=True, stop=True),
                     reads=[An_r, Xc_r], writes=[pxr])
                yield
                S.op('dve', lambda e, px=px, Xc=Xc, Xn=Xn: e.tensor_tensor(out=Xn, in0=px[0:n, 0:n], in1=Xc, op=ALU.add),
                     reads=[pxr, Xc_r], writes=[Xn_r])
                yield
                Ac, Ac_r, Bc, Bc_r, Xc, Xc_r = An, An_r, Bn, Bn_r, Xn, Xn_r
            XB, XB_r = tl('XB')
            S.op('dve', lambda e: e.tensor_scalar(out=XB, in0=Xc, scalar1=bcol, scalar2=None, op0=ALU.mult),
                 reads=[Xc_r, b_r], writes=[XB_r])
            yield
            vc, vc_r = cb[2]
            v_ = vc[:, c0:c0 + n]
            pk, pkr = bk()
            S.op('pe', lambda e: e.matmul(pk[0:n, 0:128], lhsT=k_, rhs=ID, start=True, stop=True), reads=[kn_r, 'con'], writes=[pkr])
            yield
            kend, kend_r = (T_['B0'][0][0:n, :], T_['B0'][1])
            S.op('dve', lambda e: e.tensor_scalar(out=kend, in0=pk[0:n, 0:128], scalar1=ksv[0:n, ci, hd:hd + 1], scalar2=None, op0=ALU.mult),
                 reads=[pkr, ks_r], writes=[kend_r])
            yield
            if not sample:
                pv_, pvr = bk()
                S.op('pe', lambda e: e.matmul(pv_[0:n, 0:128], lhsT=v_, rhs=ID, start=True, stop=True), reads=[vc_r, 'con'], writes=[pvr])
                yield
                vtok, vtok_r = (T_['B1'][0][0:n, :], T_['B1'][1])
                evac(vtok, pv_[0:n, 0:128], [pvr], vtok_r)
                yield
                kntok, kntok_r = (T_['X0'][0][0:n, :], T_['X0'][1])
                evac(kntok, pk[0:n, 0:128], [pkr], kntok_r)
                yield
                pU, pUr = bk()
                S.op('pe', lambda e: e.matmul(pU[0:n, 0:128], lhsT=XB, rhs=vtok, start=True, stop=True), reads=[XB_r, vtok_r], writes=[pUr])
                yield
                U0, U0_r = (T_['A0'][0][0:n, :], T_['A0'][1])
                evac(U0, pU[0:n, 0:128], [pUr], U0_r)
                yield
                XBg, XBg_r = tl('X1')
                S.op('dve', lambda e: e.tensor_scalar(out=XBg, in0=XB, scalar1=ngv[0:n, ci, hd:hd + 1], scalar2=None, op0=ALU.mult),
                     reads=[XB_r, ng_r], writes=[XBg_r])
                yield
                pW, pWr = bk()
                S.op('pe', lambda e: e.matmul(pW[0:n, 0:128], lhsT=XBg, rhs=kntok, start=True, stop=True), reads=[XBg_r, kntok_r], writes=[pWr])
                yield
                Wn, Wn_r = (T_['A1'][0][0:n, :], T_['A1'][1])
                evac(Wn, pW[0:n, 0:128], [pWr], Wn_r)
                yield
                pP, pPr = bk()
                S.op('pe', lambda e: e.matmul(pP[:, 0:128], lhsT=Wn, rhs=kend, start=True, stop=True), reads=[Wn_r, kend_r], writes=[pPr])
                yield
                TT, TT_r = (T_['X0'][0][:, :], T_['X0'][1])
                S.op('dve', lambda e: e.scalar_tensor_tensor(out=TT, in0=ID, scalar=gev[:, ci, hd:hd + 1], in1=pP[:, 0:128],
                                                             op0=ALU.mult, op1=ALU.add), reads=['con', ge_r, pPr], writes=[TT_r])
                yield
                pQ2, pQ2r = bk()
                S.op('pe', lambda e: e.matmul(pQ2[:, 0:n], lhsT=Wn, rhs=aT, start=True, stop=True), reads=[Wn_r, aT_r], writes=[pQ2r])
                yield
                S.op('dve', lambda e: e.tensor_tensor(out=qg, in0=pQ2[:, 0:n], in1=qg, op=ALU.add), reads=[pQ2r, qg_r], writes=[qg_r])
                yield

        def spath(hd, ci, n, c0, Sc, Sn, sample, T_):
            kn, kn_r = cb[1]
            vc, vc_r = cb[2]
            k_ = kn[:, c0:c0 + n]
            v_ = vc[:, c0:c0 + n]
            tl = lambda nm: (T_[nm][0][0:n, 0:n], T_[nm][1])
            aT, aT_r = tl('aT')
            XB, XB_r = tl('XB')
            qg, qg_r = (T_['qg'][0][:, 0:n], T_['qg'][1])
            Gam, Gam_r = (MT['Gam'][0][:, 0:n], MT['Gam'][1])
            kend, kend_r = (T_['B0'][0][0:n, :], T_['B0'][1])
            R, R_r = (MT['R'][0][0:n, :], MT['R'][1])
            u, u_r = (MT['u'][0][0:n, :], MT['u'][1])
            ngcol = ngv[0:n, ci, hd:hd + 1]
            if not sample:
                Sc_ap, Sc_r = Sc
                Sn_ap, Sn_r = Sn
                U0, U0_r = (T_['A0'][0][0:n, :], T_['A0'][1])
                TT, TT_r = (T_['X0'][0][:, :], T_['X0'][1])
                pS, pSr = self.bank()
                self.mm_group(pS[:, 0:128], [(TT, Sc_ap), (kend, U0)], reads=[TT_r, Sc_r, kend_r, U0_r], wres=pSr)
                evac(Sn_ap, pS[:, 0:128], [pSr], Sn_r)
                po, por = self.bank()
                self.mm_group(po[:, 0:n], [(Sc_ap, qg), (U0, aT)], reads=[Sc_r, qg_r, U0_r, aT_r], wres=por)
                evac(ob[:, c0:c0 + n], po[:, 0:n], [por], ob_r)
            else:
                sel = self.con[0:64, C_SEL:C_SEL + 16]
                gs_ap, gs_r = gsel
                gd_ap, gd_r = gends
                S.op('dve', lambda e: e.tensor_scalar(out=gs_ap[0:64, :], in0=sel, scalar1=gv[0:64, ci, hd:hd + 1], scalar2=None, op0=ALU.mult),
                     reads=['con', g_r], writes=[gs_r])
                pge, pger = self.bank()
                S.op('pe', lambda e: e.matmul(pge[:, 0:16], lhsT=self.C(C_ONES, 128, 64), rhs=gs_ap[0:64, :], start=True, stop=True),
                     reads=[gs_r, 'con'], writes=[pger])
                S.op('act', lambda e: e.activation(out=gd_ap, in_=pge[:, 0:16], func=AF.Exp), reads=[pger], writes=[gd_r])
                p1, p1r = self.bank()
                po, por = self.bank()
                def bufs(gq):
                    sin_ap, sin_r = sinB[gq % 2]
                    sout_ap, sout_r = soutB[gq % 2]
                    um_ap, um_r = umB[gq % 2]
                    return (sin_ap.rearrange("p (s v) -> p s v", v=128), sin_r,
                            sout_ap.rearrange("p (s v) -> p s v", v=128), sout_r,
                            um_ap[0:64, :].rearrange("p (s v) -> p s v", v=128), um_r)
                for gq in range(NSQ // GQ):
                    sinv, sin_r, soutv, sout_r, umv, um_r = bufs(gq)
                    S.op('sp', lambda e, gq=gq, sinv=sinv: e.dma_start(
                        out=sinv, in_=self.d_sS[ps_, di, gq * GQ:gq * GQ + GQ, hd].rearrange("s k v -> k s v")),
                        writes=[sin_r], dma=self.dq())

                    def f1(e, gq=gq, sinv=sinv):
                        ins = None
                        for s4 in range(GQ):
                            s = gq * GQ + s4
                            ins = e.matmul(p1[:, s * 4:s * 4 + 4], lhsT=sinv[:, s4, :], rhs=k_[:, s * 4:s * 4 + 4], start=True, stop=True)
                            ins = e.matmul(po[:, s * 4:s * 4 + 4], lhsT=sinv[:, s4, :], rhs=qg[:, s * 4:s * 4 + 4], start=True, stop=True)
                        return ins
                    S.op('pe', f1, reads=[sin_r, kn_r, qg_r], writes=[p1r, por])
                RT, RT_r = (T_['A0'][0][:, 0:n], T_['A0'][1])
                S.op('dve', lambda e: e.tensor_tensor(out=RT, in0=p1[:, 0:n], in1=Gam, op=ALU.mult), reads=[p1r, Gam_r], writes=[RT_r])
                S.op('dve', lambda e: e.tensor_tensor(out=RT, in0=v_, in1=RT, op=ALU.subtract), reads=[vc_r, RT_r], writes=[RT_r])
                pr_, prr = self.bank()
                S.op('pe', lambda e: e.matmul(pr_[0:n, 0:128], lhsT=RT, rhs=ID, start=True, stop=True), reads=[RT_r, 'con'], writes=[prr])
                evac(R, pr_[0:n, 0:128], [prr], R_r)
                pu_, pur = self.bank()
                S.op('pe', lambda e: e.matmul(pu_[0:n, 0:128], lhsT=XB, rhs=R, start=True, stop=True), reads=[XB_r, R_r], writes=[pur])
                evac(u, pu_[0:n, 0:128], [pur], u_r)
                evac(ob[:, c0:c0 + n], po[:, 0:n], [por], ob_r)
                po2, po2r = self.bank()
                S.op('pe', lambda e: e.matmul(po2[:, 0:n], lhsT=u, rhs=aT, start=True, stop=True),
                     reads=[u_r, aT_r], writes=[po2r])
                S.op('dve', lambda e: e.tensor_tensor(out=ob[:, c0:c0 + n], in0=po2[:, 0:n], in1=ob[:, c0:c0 + n], op=ALU.add),
                     reads=[po2r, ob_r], writes=[ob_r])
                for gq in range(NSQ // GQ):
                    sinv, sin_r, soutv, sout_r, umv, um_r = bufs(gq)
                    S.op('sp', lambda e, gq=gq, sinv=sinv: e.dma_start(
                        out=sinv, in_=self.d_sS[ps_, di, gq * GQ:gq * GQ + GQ, hd].rearrange("s k v -> k s v")),
                        writes=[sin_r], dma=self.dq())
                    for s4 in range(GQ):
                        s = gq * GQ + s4
                        S.op('dve', lambda e, s=s, s4=s4, umv=umv: e.tensor_scalar(out=umv[:, s4, :], in0=u, scalar1=sel[:, s:s + 1], scalar2=None, op0=ALU.mult),
                             reads=[u_r, 'con'], writes=[um_r])
                    for s4 in range(GQ):
                        s = gq * GQ + s4
                        pS, pSr = self.bank()
                        S.op('pe', lambda e, pS=pS, s4=s4, umv=umv: e.matmul(pS[:, 0:128], lhsT=kend, rhs=umv[:, s4, :], start=True, stop=True),
                             reads=[kend_r, um_r], writes=[pSr])
                        S.op('dve', lambda e, pS=pS, s=s, s4=s4, soutv=soutv, sinv=sinv: e.scalar_tensor_tensor(
                            out=soutv[:, s4, :], in0=sinv[:, s4, :], scalar=gd_ap[:, s:s + 1], in1=pS[:, 0:128], op0=ALU.mult, op1=ALU.add),
                            reads=[sin_r, gd_r, pSr], writes=[sout_r])
                    S.op('sp', lambda e, gq=gq, soutv=soutv: e.dma_start(
                        out=self.o_sS[ps_, di, gq * GQ:gq * GQ + GQ, hd].rearrange("s k v -> k s v"), in_=soutv),
                        reads=[sout_r], dma=self.dq())

        for hd in range(12):
            wq = [self.panel(self.wview(W, j * 1536 + hd * 128, 128), 16, 128) for j in range(3)]
            wz = self.panel(self.wview(W, 4608 + hd * 128, 128), 16, 128)
            cidx = [hd, 12 + hd, 24 + hd]
            for j in range(3):
                p_ap, p_r = pads[j]
                s_ap, s_r = spads[j]
                sv = s_ap.rearrange("p (s r) -> p s r", r=7)
                if ps_ == 0:
                    S.op('dve', lambda e, p_ap=p_ap: e.memset(p_ap[:, 0:3], 0.0), writes=[p_r])
                else:
                    S.op('sp', lambda e, cidx=cidx, p_ap=p_ap, j=j: e.dma_start(out=p_ap[:, 0:3], in_=self.c_conv[di, cidx[j]]),
                         reads=['c_conv%d_%d' % (di, cidx[j])], writes=[p_r], dma=self.dq())
                S.op('sp', lambda e, cidx=cidx, sv=sv, j=j: e.dma_start(
                    out=sv[:, :, 0:3], in_=self.d_sconv[ps_, di, cidx[j]].rearrange("p (s r) -> p s r", r=3)),
                    writes=[s_r], dma=self.dq())
            S0, S0_r = Sb[0]
            if ps_ == 0:
                S.op('dve', lambda e: e.memset(S0, 0.0), writes=[S0_r])
            else:
                S.op('sp', lambda e, hd=hd: e.dma_start(out=S0, in_=self.c_S[di, hd]), reads=['c_S%d_%d' % (di, hd)],
                     writes=[S0_r], dma=self.dq())
            scur = 0
            for ti, (t0, tn, tnp) in enumerate(TT):
                def stream_gen(j, bk, ti=ti, t0=t0, tn=tn, tnp=tnp):
                    t1, t1_r = t1s[min(j, 1)]
                    wp, wr = wq[j]
                    p_ap, p_r = pads[j]
                    s_ap, s_r = spads[j]
                    sv = s_ap.rearrange("p (s r) -> p s r", r=7)
                    c_ap, c_r = cb[j]
                    pb, pr = bk()
                    self.mm_group(pb[:, 0:tn], [(wp[:, k, :], self.hT[:, k, t0:t0 + tn]) for k in range(16)],
                                  reads=[wr] + hreads, wres=pr)
                    yield
                    S.op('act', lambda e, pb=pb, p_ap=p_ap, tnp=tnp: e.copy(out=p_ap[:, 3:3 + tnp], in_=pb[:, 0:tnp]),
                         reads=[pr], writes=[p_r])
                    yield
                    cw = lambda tap, cidx=cidx, j=j: self.Pc(P_CONVW + (di * 36 + cidx[j]) * 4 + tap)
                    S.op('dve', lambda e, c_ap=c_ap, p_ap=p_ap, tnp=tnp, cw=cw: e.tensor_scalar(
                        out=c_ap[:, 0:tnp], in0=p_ap[:, 3:3 + tnp], scalar1=cw(3), scalar2=None, op0=ALU.mult),
                        reads=[p_r, 'par'], writes=[c_r])
                    yield
                    for tap in range(3):
                        S.op('dve', lambda e, c_ap=c_ap, p_ap=p_ap, tnp=tnp, cw=cw, tap=tap: e.scalar_tensor_tensor(
                            out=c_ap[:, 0:tnp], in0=p_ap[:, tap:tap + tnp], scalar=cw(tap), in1=c_ap[:, 0:tnp],
                            op0=ALU.mult, op1=ALU.add), reads=[p_r, c_r, 'par'], writes=[c_r])
                        yield
                    if ti == 2:
                        cs = c_ap[:, 256:320].rearrange("p (s q) -> p s q", q=4)
                        S.op('act', lambda e, pb=pb, sv=sv: e.copy(out=sv[:, :, 3:7], in_=pb[:, 256:320].rearrange("p (s q) -> p s q", q=4)),
                             reads=[pr], writes=[s_r])
                        yield
                        S.op('dve', lambda e, cs=cs, sv=sv, cw=cw: e.tensor_scalar(
                            out=cs, in0=sv[:, :, 3:7], scalar1=cw(3), scalar2=None, op0=ALU.mult), reads=[s_r, 'par'], writes=[c_r])
                        yield
                        for tap in range(3):
                            S.op('dve', lambda e, cs=cs, sv=sv, cw=cw, tap=tap: e.scalar_tensor_tensor(
                                out=cs, in0=sv[:, :, tap:tap + 4], scalar=cw(tap), in1=cs, op0=ALU.mult, op1=ALU.add),
                                reads=[s_r, c_r, 'par'], writes=[c_r])
                            yield
                        S.op('sp', lambda e, cidx=cidx, sv=sv, j=j: e.dma_start(
                            out=self.o_sconv[ps_, di, cidx[j]].rearrange("p (s r) -> p s r", r=3), in_=sv[:, :, 4:7]),
                            reads=[s_r], dma=self.dq())
                        if ps_ == 0:
                            S.op('sp', lambda e, cidx=cidx, p_ap=p_ap, j=j, tnp=tnp: e.dma_start(out=self.c_conv[di, cidx[j]], in_=p_ap[:, tnp:tnp + 3]),
                                 reads=[p_r], writes=['c_conv%d_%d' % (di, cidx[j])], dma=self.dq())
                        else:
                            S.op('sp', lambda e, cidx=cidx, p_ap=p_ap, j=j, tnp=tnp: e.dma_start(out=self.o_pconv[di, cidx[j]], in_=p_ap[:, tnp:tnp + 3]),
                                 reads=[p_r], dma=self.dq())
                    else:
                        S.op('act', lambda e, p_ap=p_ap, tnp=tnp: e.copy(out=p_ap[:, 0:3], in_=p_ap[:, tnp:tnp + 3]),
                             reads=[p_r], writes=[p_r])
                    yield
                    tb, tb_r = (t1, t1_r) if j < 2 else (ob, ob_r)
                    S.op('act', lambda e, c_ap=c_ap, tn=tn, tb=tb: e.activation(out=tb[:, 0:tn], in_=c_ap[:, 0:tn], func=AF.Exp, scale=-1.0),
                         reads=[c_r], writes=[tb_r])
                    yield
                    S.op('act', lambda e, tn=tn, tb=tb: e.activation(out=tb[:, 0:tn], in_=tb[:, 0:tn], func=AF.Ln, bias=1.0),
                         reads=[tb_r], writes=[tb_r])
                    yield
                    S.op('act', lambda e, tn=tn, tb=tb: e.activation(out=tb[:, 0:tn], in_=tb[:, 0:tn], func=AF.Exp, scale=-1.0),
                         reads=[tb_r], writes=[tb_r])
                    yield
                    S.op('dve', lambda e, c_ap=c_ap, tn=tn, tb=tb: e.tensor_tensor(out=c_ap[:, 0:tn], in0=c_ap[:, 0:tn], in1=tb[:, 0:tn], op=ALU.mult),
                         reads=[c_r, tb_r], writes=[c_r])
                    yield
                    if j < 2:
                        S.op('act', lambda e, c_ap=c_ap, tn=tn: e.activation(out=t1[:, 0:tn], in_=c_ap[:, 0:tn], func=AF.Square),
                             reads=[c_r], writes=[t1_r])
                        yield
                        pn, pnr = bk()
                        S.op('pe', lambda e, pn=pn, tn=tn: e.matmul(pn[:, 0:tn], lhsT=self.C(C_ONES), rhs=t1[:, 0:tn], start=True, stop=True),
                             reads=[t1_r, 'con'], writes=[pnr])
                        yield
                        self.rsqrt(t1[:, 0:tn], t1_r, pn[:, 0:tn], pnr)
                        yield
                        sc_ = (128.0 ** -0.5) if j == 0 else 1.0
                        S.op('dve', lambda e, c_ap=c_ap, tn=tn, sc_=sc_: e.scalar_tensor_tensor(
                            out=c_ap[:, 0:tn], in0=c_ap[:, 0:tn], scalar=sc_, in1=t1[:, 0:tn], op0=ALU.mult, op1=ALU.mult),
                            reads=[c_r, t1_r], writes=[c_r])
                        yield
                def z_gen(bk, tn=tn, t0=t0):
                  wp, wr = wz
                  pb, pr = bk()
                  if True:
                    self.mm_group(pb[:, 0:tn], [(wp[:, k, :], self.hT[:, k, t0:t0 + tn]) for k in range(16)], reads=[wr] + hreads, wres=pr)
                    yield
                    S.op('act', lambda e, pb=pb, tn=tn: e.activation(out=zs[:, 0:tn], in_=pb[:, 0:tn], func=AF.Exp, scale=-1.0), reads=[pr], writes=[zs_r])
                    yield
                    S.op('act', lambda e, tn=tn: e.activation(out=zs[:, 0:tn], in_=zs[:, 0:tn], func=AF.Ln, bias=1.0), reads=[zs_r], writes=[zs_r])
                    yield
                    S.op('act', lambda e, tn=tn: e.activation(out=zs[:, 0:tn], in_=zs[:, 0:tn], func=AF.Exp, scale=-1.0), reads=[zs_r], writes=[zs_r])
                    yield
                    S.op('dve', lambda e, pb=pb, tn=tn: e.tensor_tensor(out=zs[:, 0:tn], in0=pb[:, 0:tn], in1=zs[:, 0:tn], op=ALU.mult),
                         reads=[pr, zs_r], writes=[zs_r])
                    yield
                gens2 = [stream_gen(j, bankfn(j)) for j in range(3)] + [z_gen(bankfn(3))]
                while gens2:
                    for g_ in list(gens2):
                        try:
                            next(g_)
                        except StopIteration:
                            gens2.remove(g_)
                if ti == 2:
                    self.release(4)
                nch = 3 if ti < 2 else 2
                gens = [prep_gen(hd, ti * 3 + cj, 128, cj * 128, False, CT[cj], bankfn(cj)) for cj in range(nch)]
                if ti == 2:
                    gens.append(prep_gen(hd, 8, 64, 256, True, CT[2], bankfn(2)))
                if 'prep' in RUN_SKIP:
                    gens = []
                while gens:
                    for g_ in list(gens):
                        try:
                            next(g_)
                        except StopIteration:
                            gens.remove(g_)
                for cj in range(nch if 'spath' not in RUN_SKIP else 0):
                    spath(hd, ti * 3 + cj, 128, cj * 128, Sb[scur], Sb[1 - scur], False, CT[cj])
                    scur = 1 - scur
                if ti == 2 and 'spath' not in RUN_SKIP:
                    spath(hd, 8, 64, 256, None, None, True, CT[2])
                S.op('act', lambda e, tn=tn: e.activation(out=t1[:, 0:tn], in_=ob[:, 0:tn], func=AF.Square), reads=[ob_r], writes=[t1_r])
                pn, pnr = self.bank()
                S.op('pe', lambda e, pn=pn, tn=tn: e.matmul(pn[:, 0:tn], lhsT=self.C(C_ONES), rhs=t1[:, 0:tn], start=True, stop=True),
                     reads=[t1_r, 'con'], writes=[pnr])
                self.rsqrt(t1[:, 0:tn], t1_r, pn[:, 0:tn], pnr, scale=1.0 / 128.0)
                S.op('dve', lambda e, tn=tn: e.scalar_tensor_tensor(out=ob[:, 0:tn], in0=ob[:, 0:tn], scalar=self.Pc(P_ONORM + di),
                                                                    in1=t1[:, 0:tn], op0=ALU.mult, op1=ALU.mult),
                     reads=[ob_r, t1_r, 'par'], writes=[ob_r])
                S.op('dve', lambda e, tn=tn, t0=t0, hd=hd: e.tensor_tensor(out=self.yb[:, hd, t0:t0 + tn], in0=ob[:, 0:tn], in1=zs[:, 0:tn], op=ALU.mult),
                     reads=[ob_r, zs_r], writes=['y%d' % hd])
            Sf, Sf_r = Sb[scur]
            if ps_ == 0:
                S.op('sp', lambda e, Sf=Sf, hd=hd: e.dma_start(out=self.c_S[di, hd], in_=Sf), reads=[Sf_r],
                     writes=['c_S%d_%d' % (di, hd)], dma=self.dq())
            else:
                S.op('sp', lambda e, Sf=Sf, hd=hd: e.dma_start(out=self.o_pS[di, hd], in_=Sf), reads=[Sf_r], dma=self.dq())
        self.phase()
        qsv, qs_r = self.qmem_and_prompt_attn(lambda hm: self.wview(W, 6168 + hm * 128, 128))
        if 'sattn' not in RUN_SKIP:
            self.xattn_sample(l, ps_, qsv, qs_r)

    def program(self):
        S = self.S
        S.op('sp', lambda e: e.dma_start(out=self.par, in_=self.d_par), writes=['par'], dma=self.dq())
        S.op('sp', lambda e: e.dma_start(out=self.con, in_=self.d_con), writes=['con'], dma=self.dq())
        S.op('dve', lambda e: e.memset(self.onesb, 1.0), writes=['onesb'])
        for ps_ in range(RUN_NPASS):
            for c in range(16):
                S.op('sp', lambda e, c=c, ps_=ps_: e.dma_start(out=self.xT[:, c, :], in_=self.d_x[ps_, c * 128:(c + 1) * 128, :]),
                     writes=['x%d_%d' % (c, ti) for ti in range(3)], dma=self.dq())
            for l in range(RUN_DEPTH):
                if 'memkv' not in RUN_SKIP:
                    self.mem_kv(l, write_out=(ps_ == 0))
                self.rmsnorm_to_h(P_GMIX + l * 16)
                if 'mixer' not in RUN_SKIP:
                    if l % 2 == 0:
                        self.delta_layer(l, ps_)
                    else:
                        self.pool_layer(l, ps_)
                if self.dbg and l == RUN_DEPTH - 1 and ps_ == 0:
                    S.op('sp', lambda e: e.dma_start(out=self.o_dbg, in_=self.yb), reads=['y%d' % c for c in range(16)], dma=self.dq())
                if 'wout' not in RUN_SKIP:
                    self.gemm_x_update(lambda m, l=l: self.wview(self.w_out[l], m * 128, 128), 16,
                                       lambda k, t0, tn: self.yb[:, k, t0:t0 + tn], ['y%d' % c for c in range(16)])
                self.rmsnorm_to_h(P_GFFN + l * 16)
                self.phase()
                self.sgbuf = [self.alloc("sg%d" % i, 384) for i in range(2)]
                if 'ffn' not in RUN_SKIP:
                    self.ffn(l)
            self.final_norm_out(ps_)

    def final_norm_out(self, ps_):
        S = self.S
        self.phase()
        sq, sq_r = self.alloc("fsq", T)
        rs, rs_r = self.alloc("frs", T)
        st = [self.alloc("fst%d" % i, T) for i in range(2)]
        xr = lambda c, ti: 'x%d_%d' % (c, ti)
        banks = [self.bank() for _ in TT]
        for c in range(16):
            S.op('act', lambda e, c=c: e.activation(out=sq, in_=self.xT[:, c, :], func=AF.Square),
                 reads=[xr(c, ti) for ti in range(3)], writes=[sq_r])
            for ti, (t0, tn, _) in enumerate(TT):
                pb, pr = banks[ti]
                S.op('pe', lambda e, pb=pb, t0=t0, tn=tn, c=c: e.matmul(
                    pb[:, 0:tn], lhsT=self.C(C_AVG), rhs=sq[:, t0:t0 + tn], start=(c == 0), stop=(c == 15)),
                    reads=[sq_r, 'con'], writes=[pr])
        for ti, (t0, tn, _) in enumerate(TT):
            pb, pr = banks[ti]
            self.rsqrt(rs[:, t0:t0 + tn], rs_r, pb[:, 0:tn], pr)
        for c in range(16):
            s_ap, s_r = st[c % 2]
            S.op('dve', lambda e, c=c, s_ap=s_ap: e.scalar_tensor_tensor(
                out=s_ap, in0=self.xT[:, c, :], scalar=self.Pc(P_GFIN + c), in1=rs, op0=ALU.mult, op1=ALU.mult),
                reads=[xr(c, ti) for ti in range(3)] + [rs_r, 'par'], writes=[s_r])
            S.op('sp', lambda e, c=c, s_ap=s_ap: e.dma_start(out=self.o_y[ps_, c * 128:(c + 1) * 128, :], in_=s_ap),
                 reads=[s_r], dma=self.dq())


_CACHE = {}
ACTIVE = [0, 2, 4, 6]


def build_nc():
    if 'nc' in _CACHE:
        return _CACHE['nc']
    b0 = Builder(plan=None)
    b0.build()
    b1 = Builder(plan=b0.rec)
    nc = b1.build()
    assert b1.ws_i == len(b0.rec)
    _CACHE['nc'] = nc
    return nc


def make_consts():
    c = np.zeros((128, NC_), np.float32)
    i = np.arange(128)
    c[:, C_ID:C_ID + 128] = np.eye(128, dtype=np.float32)
    c[:, C_ONES:C_ONES + 128] = 1.0
    c[:, C_AVG:C_AVG + 128] = 1.0 / 2048.0
    U = (i[:, None] <= i[None, :]).astype(np.float32)
    c[:, C_U:C_U + 128] = U
    SL = (i[:, None] > i[None, :])
    c[:, C_MBSL:C_MBSL + 128] = np.where(SL, 0.0, 1e30)
    UI = (i[None, :] >= i[:, None])
    c[:, C_MNUI:C_MNUI + 128] = np.where(UI, 0.0, -1e30)
    seq = i // 4
    same = (seq[:, None] == seq[None, :])
    c[:, C_US:C_US + 128] = (same & (i[:, None] <= i[None, :])).astype(np.float32)
    c[:, C_BMS:C_BMS + 128] = same.astype(np.float32)
    c[:, C_MBSLS:C_MBSLS + 128] = np.where(same & SL, 0.0, 1e30)
    c[:, C_MNUIS:C_MNUIS + 128] = np.where(same & UI, 0.0, -1e30)
    sel = np.zeros((128, 16), np.float32)
    sel[np.arange(64), np.arange(64) // 4] = 1.0
    c[:, C_SEL:C_SEL + 16] = sel
    c[:, C_EPS] = EPS
    return c


def make_params(inp):
    p = np.zeros((128, NP_), np.float32)
    fm = lambda a: np.ascontiguousarray(a.reshape(a.shape[0], -1, 128).transpose(2, 0, 1)).reshape(128, -1)
    p[:, P_GMIX:P_GMIX + 64] = fm(inp['norm_mix'])
    p[:, P_GFFN:P_GFFN + 64] = fm(inp['norm_ffn'])
    p[:, P_GMEM:P_GMEM + 64] = fm(inp['norm_mem'])
    p[:, P_GFIN:P_GFIN + 16] = fm(inp['norm_final'][None, :])
    cw = inp['conv_w']
    p[:, P_CONVW:P_CONVW + 288] = np.ascontiguousarray(cw.reshape(2, 4, 36, 128).transpose(3, 0, 2, 1)).reshape(128, 288)
    p[:, P_PSCALE:P_PSCALE + 24] = fm(inp['pool_scale'])
    p[:, P_ONORM:P_ONORM + 2] = inp['delta_onorm'].T
    p[:, P_ALOG:P_ALOG + 24] = np.broadcast_to(inp['a_log'].reshape(1, 24), (128, 24))
    p[:, P_DTB:P_DTB + 24] = np.broadcast_to(inp['dt_bias'].reshape(1, 24), (128, 24))
    wins = np.array([2, 4, 8, 16], np.float32)
    t = np.arange(16, dtype=np.float32)
    cnt0 = np.minimum(wins[:, None], t[None, :] + 1.0)
    cnt1 = np.broadcast_to(wins[:, None], (4, 16))
    tab = np.stack([1.0 / cnt0, 1.0 / cnt1]).astype(np.float32)
    p[:, P_CNT:P_CNT + 128] = np.broadcast_to(tab.reshape(1, 128), (128, 128))
    return p


def make_in_map(inp, core, consts, params):
    b = core % 4
    m = {}
    xT = np.empty((NPASS, D, T), np.float32)
    for ps_ in range(NPASS):
        xT[ps_, :, :TP] = inp['x_prompt'][b, ps_ * TP:(ps_ + 1) * TP, :].T
        sq0 = b * 32 + ps_ * NSQ
        xT[ps_, :, TP:] = inp['x_sample'][sq0:sq0 + NSQ].reshape(TS, D).T
    m['xT'] = xT
    m['memT'] = np.ascontiguousarray(inp['mem_prompt'][b].T)
    sl = [slice(b * 32 + ps_ * NSQ, b * 32 + (ps_ + 1) * NSQ) for ps_ in range(NPASS)]
    m['sS'] = np.stack([inp['state_delta_S'][:, s] for s in sl])
    sc = np.stack([inp['state_delta_conv'][:, s] for s in sl])
    m['sconvT'] = np.ascontiguousarray(sc.reshape(NPASS, 2, NSQ, 3, 36, 128).transpose(0, 1, 4, 5, 2, 3)).reshape(NPASS, 2, 36, 128, NSQ * 3)
    sp = np.stack([inp['state_pool'][:, s] for s in sl])
    m['spoolT'] = np.ascontiguousarray(sp.reshape(NPASS, 2, NSQ, 15, 12, 128).transpose(0, 1, 4, 5, 2, 3)).reshape(NPASS, 2, 12, 128, NSQ * 15)
    ck = np.stack([inp['cache_mem_k'][:, s] for s in sl])
    m['ckT'] = np.ascontiguousarray(ck.transpose(0, 1, 2, 5, 4, 3)).reshape(NPASS, DEPTH, NSQ, 128, 1024)
    m['cv'] = np.stack([inp['cache_mem_v'][:, s] for s in sl]).reshape(NPASS, DEPTH, NSQ, 256, 512)
    m['params'] = params
    m['consts'] = consts
    for k in ('w_mem_kv', 'w_out', 'w_gate_up', 'w_down'):
        m[k] = inp[k][:RUN_DEPTH]
    m['w_in_delta'] = inp['w_in_delta'][:W_ND]
    m['w_in_pool'] = inp['w_in_pool'][:W_NP]
    m['w_pool_grp'] = inp['w_pool_grp'][:W_NP]
    return m


def kernel(**inp):
    inp = {k: np.asarray(v) for k, v in inp.items()}
    nc = build_nc()
    consts = make_consts()
    params = make_params(inp)
    maps_b = [make_in_map(inp, b, consts, params) for b in range(4)]
    zero_map = {k: np.zeros_like(v) for k, v in maps_b[0].items()}
    in_maps = [maps_b[ACTIVE.index(c)] if c in ACTIVE else zero_map for c in range(8)]
    res = run_bass_kernel_spmd(nc, in_maps, core_ids=list(range(8)))
    R = res.results
    y_prompt = np.empty((4, 2048, D), np.float32)
    y_sample = np.empty((128, 4, D), np.float32)
    p_S = np.empty((2, 4, 12, 128, 128), np.float32)
    p_conv = np.empty((2, 4, 3, 4608), np.float32)
    p_pool = np.empty((2, 4, 15, 1536), np.float32)
    p_mk = np.empty((4, 4, 256, 4, 128), np.float32)
    p_mv = np.empty((4, 4, 256, 4, 128), np.float32)
    s_S = np.empty((2, 128, 12, 128, 128), np.float32)
    s_conv = np.empty((2, 128, 3, 4608), np.float32)
    s_pool = np.empty((2, 128, 15, 1536), np.float32)
    for b in range(4):
        r = R[ACTIVE[b]]
        for ps_ in range(NPASS):
            yT = r['yT'][ps_]
            y_prompt[b, ps_ * TP:(ps_ + 1) * TP, :] = yT[:, :TP].T
            sq0 = b * 32 + ps_ * NSQ
            y_sample[sq0:sq0 + NSQ] = yT[:, TP:].T.reshape(NSQ, 4, D)
            s_S[:, sq0:sq0 + NSQ] = r['o_sS'][ps_]
            s_conv[:, sq0:sq0 + NSQ] = r['o_sconvT'][ps_].reshape(2, 36, 128, NSQ, 3).transpose(0, 3, 4, 1, 2).reshape(2, NSQ, 3, 4608)
            s_pool[:, sq0:sq0 + NSQ] = r['o_spoolT'][ps_].reshape(2, 12, 128, NSQ, 15).transpose(0, 3, 4, 1, 2).reshape(2, NSQ, 15, 1536)
        p_S[:, b] = r['o_pS']
        p_conv[:, b] = r['o_pconvT'].transpose(0, 3, 1, 2).reshape(2, 3, 4608)
        p_pool[:, b] = r['o_ppoolT'].transpose(0, 3, 1, 2).reshape(2, 15, 1536)
        p_mk[:, b] = r['o_mkT'].reshape(4, 128, 4, 256).transpose(0, 3, 2, 1)
        p_mv[:, b] = r['o_mv'].reshape(4, 256, 4, 128)
    return (y_prompt, y_sample, p_S, p_conv, p_pool, p_mk, p_mv, s_S, s_conv, s_pool)
```
